# Optimizing a Trainium2 kernel written in Bass

```python
import math
import jax, jax.numpy as jnp
from jax import lax
import numpy as np

D_MODEL = 4096
BATCH = 1
SEQ = 16384
DEPTH = 2

N_MIXERS = 2
N_CONV_LAYERS = (DEPTH + 1) // 2
N_ATTN_LAYERS = DEPTH // 2

CONV_WIDTH = 31

HEAD_DIM = 128
N_HEADS = D_MODEL // HEAD_DIM
N_KV_HEADS = 8
GROUP = N_HEADS // N_KV_HEADS
WINDOW = 128
BLOCK = 128
QKV_DIM = (N_HEADS + 2 * N_KV_HEADS) * HEAD_DIM
ATTN_SCALE = 1.0 / math.sqrt(HEAD_DIM)

NUM_BUCKETS = 32
MAX_EXACT = NUM_BUCKETS // 2
MAX_DISTANCE = 128

D_FF = 11008
FFN_CONV_WIDTH = 3

EPS = 1e-6

kernel_name = "interleaved_conformer_conv_swa_sink_hybrid"


def rms_norm(x, g):
    xf = x.astype(jnp.float32)
    y = xf * lax.rsqrt(jnp.mean(xf * xf, axis=-1, keepdims=True) + EPS)
    return (y * g.astype(jnp.float32)).astype(x.dtype)


def layer_norm(x, g, b):
    xf = x.astype(jnp.float32)
    mu = jnp.mean(xf, axis=-1, keepdims=True)
    xc = xf - mu
    y = xc * lax.rsqrt(jnp.mean(xc * xc, axis=-1, keepdims=True) + EPS)
    return (y * g.astype(jnp.float32) + b.astype(jnp.float32)).astype(x.dtype)


def causal_depthwise_conv(x, w, b):
    width = w.shape[0]
    y = lax.conv_general_dilated(
        x, w[:, None, :].astype(x.dtype), window_strides=(1,),
        padding=[(width - 1, 0)], dimension_numbers=("NWC", "WIO", "NWC"),
        feature_group_count=x.shape[-1])
    return y + b.astype(x.dtype)


def t5_causal_bucket(dist):
    n = jnp.maximum(dist, 0)
    is_small = n < MAX_EXACT
    nf = jnp.maximum(n, 1).astype(jnp.float32)
    large = MAX_EXACT + (jnp.log(nf / MAX_EXACT) / math.log(MAX_DISTANCE / MAX_EXACT)
                         * (NUM_BUCKETS - MAX_EXACT)).astype(jnp.int32)
    large = jnp.minimum(large, NUM_BUCKETS - 1)
    return jnp.where(is_small, n, large)


def conformer_conv(h, pw1_w, pw1_b, dw_w, dw_b, ln_g, ln_b, pw2_w, pw2_b):
    a, g = jnp.split(h @ pw1_w + pw1_b, 2, axis=-1)
    u = a * jax.nn.sigmoid(g)
    u = causal_depthwise_conv(u, dw_w, dw_b)
    u = jax.nn.silu(layer_norm(u, ln_g, ln_b))
    return u @ pw2_w + pw2_b


def sliding_window_attention(h, w_qkv, q_g, k_g, sinks, w_o, rel_bias):
    B, T, _ = h.shape
    nb = T // BLOCK
    qkv = h @ w_qkv
    q, k, v = jnp.split(qkv, [N_HEADS * HEAD_DIM, (N_HEADS + N_KV_HEADS) * HEAD_DIM], axis=-1)
    q = rms_norm(q.reshape(B, T, N_HEADS, HEAD_DIM), q_g)
    k = rms_norm(k.reshape(B, T, N_KV_HEADS, HEAD_DIM), k_g)
    v = v.reshape(B, T, N_KV_HEADS, HEAD_DIM)
    q = q.reshape(B, nb, BLOCK, N_KV_HEADS, GROUP, HEAD_DIM)

    def band(t):
        t = t.reshape(B, nb, BLOCK, N_KV_HEADS, HEAD_DIM)
        prev = jnp.concatenate([jnp.zeros_like(t[:, :1]), t[:, :-1]], axis=1)
        return jnp.concatenate([prev, t], axis=2)

    kb, vb = band(k), band(v)
    qi = jnp.arange(BLOCK, dtype=jnp.int32)[:, None]
    kj = jnp.arange(2 * BLOCK, dtype=jnp.int32)[None, :]
    dist = qi - kj + BLOCK
    in_window = (dist >= 0) & (dist < WINDOW)
    kpos = jnp.arange(nb, dtype=jnp.int32)[:, None] * BLOCK - BLOCK + kj
    mask = in_window[None] & (kpos >= 0)[:, None, :]

    bias = rel_bias.astype(jnp.float32)[t5_causal_bucket(dist)]
    bias = bias.transpose(2, 0, 1).reshape(N_KV_HEADS, GROUP, BLOCK, 2 * BLOCK)

    s = jnp.einsum("bnqkgd,bnskd->bnkgqs", q, kb).astype(jnp.float32) * ATTN_SCALE + bias
    s = jnp.where(mask[None, :, None, None], s, -jnp.inf)
    sink = sinks.astype(jnp.float32).reshape(N_KV_HEADS, GROUP)[:, :, None, None]
    m = jnp.maximum(jnp.max(s, axis=-1, keepdims=True), sink)
    p = jnp.exp(s - m)
    denom = jnp.sum(p, axis=-1, keepdims=True) + jnp.exp(sink - m)
    p = (p / denom).astype(vb.dtype)
    o = jnp.einsum("bnkgqs,bnskd->bnqkgd", p, vb).reshape(B, T, N_HEADS * HEAD_DIM)
    return o @ w_o


def conv_gated_ffn(h, w_in, dw_w, dw_b, w_out):
    gate, val = jnp.split(h @ w_in, 2, axis=-1)
    gate = causal_depthwise_conv(gate, dw_w, dw_b)
    return (jax.nn.silu(gate) * val) @ w_out


def setup_inputs(seed: int = 0) -> dict:
    key = jax.random.key(seed)
    ks = jax.random.split(key, 24)
    f32 = jnp.float32

    def nrm(k, shape, scale):
        return jax.random.normal(k, shape, f32) * scale

    NC, NA = N_CONV_LAYERS, N_ATTN_LAYERS
    D, F = D_MODEL, D_FF
    return {
        "x": nrm(ks[0], (BATCH, SEQ, D), 1.0),
        "mix_norm_g": 1.0 + nrm(ks[1], (DEPTH, D), 0.1),
        "conv_pw1_w": nrm(ks[2], (NC, D, 2 * D), D ** -0.5),
        "conv_pw1_b": nrm(ks[3], (NC, 2 * D), 0.02),
        "conv_dw_w": nrm(ks[4], (NC, CONV_WIDTH, D), CONV_WIDTH ** -0.5),
        "conv_dw_b": nrm(ks[5], (NC, D), 0.02),
        "conv_ln_g": 1.0 + nrm(ks[6], (NC, D), 0.1),
        "conv_ln_b": nrm(ks[7], (NC, D), 0.02),
        "conv_pw2_w": nrm(ks[8], (NC, D, D), D ** -0.5),
        "conv_pw2_b": nrm(ks[9], (NC, D), 0.02),
        "attn_w_qkv": nrm(ks[10], (NA, D, QKV_DIM), D ** -0.5),
        "attn_q_norm_g": 1.0 + nrm(ks[11], (NA, HEAD_DIM), 0.1),
        "attn_k_norm_g": 1.0 + nrm(ks[12], (NA, HEAD_DIM), 0.1),
        "attn_sinks": nrm(ks[13], (NA, N_HEADS), 1.0),
        "attn_w_o": nrm(ks[14], (NA, N_HEADS * HEAD_DIM, D), (N_HEADS * HEAD_DIM) ** -0.5),
        "rel_bias": nrm(ks[15], (NUM_BUCKETS, N_HEADS), 0.5),
        "ffn_norm_g": 1.0 + nrm(ks[16], (DEPTH, D), 0.1),
        "ffn_w_in": nrm(ks[17], (DEPTH, D, 2 * F), D ** -0.5),
        "ffn_dw_w": nrm(ks[18], (DEPTH, FFN_CONV_WIDTH, F), FFN_CONV_WIDTH ** -0.5),
        "ffn_dw_b": nrm(ks[19], (DEPTH, F), 0.02),
        "ffn_w_out": nrm(ks[20], (DEPTH, F, D), F ** -0.5),
    }


def reference(x, mix_norm_g, conv_pw1_w, conv_pw1_b, conv_dw_w, conv_dw_b, conv_ln_g,
              conv_ln_b, conv_pw2_w, conv_pw2_b, attn_w_qkv, attn_q_norm_g, attn_k_norm_g,
              attn_sinks, attn_w_o, rel_bias, ffn_norm_g, ffn_w_in, ffn_dw_w, ffn_dw_b,
              ffn_w_out):
    for i in range(DEPTH):
        h = rms_norm(x, mix_norm_g[i])
        j = i // N_MIXERS
        if i % N_MIXERS == 0:
            y = conformer_conv(h, conv_pw1_w[j], conv_pw1_b[j], conv_dw_w[j], conv_dw_b[j],
                               conv_ln_g[j], conv_ln_b[j], conv_pw2_w[j], conv_pw2_b[j])
        else:
            y = sliding_window_attention(h, attn_w_qkv[j], attn_q_norm_g[j], attn_k_norm_g[j],
                                         attn_sinks[j], attn_w_o[j], rel_bias)
        x = x + y
        h = rms_norm(x, ffn_norm_g[i])
        x = x + conv_gated_ffn(h, ffn_w_in[i], ffn_dw_w[i], ffn_dw_b[i], ffn_w_out[i])
    return x
```

```python
import contextlib
import math
import numpy as np
import ml_dtypes
import concourse.bass as bass
import concourse.mybir as mybir
from concourse.bass_utils import run_bass_kernel_spmd

F32 = mybir.dt.float32
BF16 = mybir.dt.bfloat16
AF = mybir.ActivationFunctionType
ALU = mybir.AluOpType

ENGS = ("pe", "act", "dve", "pool", "sp")
EPS = 1e-6
CW = 31
FW = 3
NEG = -30000.0


class Buf:
    __slots__ = ("name", "w", "r", "excl")

    def __init__(self, name, excl=False):
        self.name = name
        self.w = None
        self.r = {}
        self.excl = excl


class Op:
    __slots__ = ("eng", "fn", "deps", "flag", "semval", "dsem", "is_dma")

    def __init__(self, eng, fn, is_dma):
        self.eng = eng
        self.fn = fn
        self.deps = []
        self.flag = False
        self.semval = 0
        self.dsem = None
        self.is_dma = is_dma


class Prog:
    def __init__(self, n_dma_sems=16):
        self.q = {e: [] for e in ENGS}
        self.n_dma_sems = n_dma_sems
        self.dma_last = {e: [None] * n_dma_sems for e in ENGS}
        self.dma_uses = {e: [0] * n_dma_sems for e in ENGS}
        self.dma_rr = {e: 0 for e in ENGS}

    def _collect(self, op, reads, writes):
        deps = {}
        for b in reads:
            if b.w is not None:
                deps[id(b.w)] = b.w
            if b.excl:
                for k_, r in b.r.items():
                    if r.eng != op.eng:
                        deps[id(r)] = r
        for b in writes:
            if b.w is not None:
                deps[id(b.w)] = b.w
            for r in b.r.values():
                deps[id(r)] = r
        for d in deps.values():
            if d is op:
                continue
            if d.eng == "pe" and op.eng == "pe" and not d.is_dma and not op.is_dma:
                continue
            op.deps.append(d)
            if not d.is_dma:
                d.flag = True
        key = op.eng + ("_d" if op.is_dma else "")
        for b in reads:
            b.r[key] = op
        for b in writes:
            b.w = op
            b.r = {}

    def op(self, eng, fn, reads=(), writes=()):
        o = Op(eng, fn, False)
        self._collect(o, reads, writes)
        self.q[eng].append(o)
        return o

    def dma(self, eng, fn, reads=(), writes=()):
        o = Op(eng, fn, True)
        s = self.dma_rr[eng]
        self.dma_rr[eng] = (s + 1) % self.n_dma_sems
        prev = self.dma_last[eng][s]
        self._collect(o, reads, writes)
        if prev is not None and prev not in o.deps:
            o.deps.append(prev)
        self.dma_uses[eng][s] += 1
        o.dsem = (eng, s)
        o.semval = 16 * self.dma_uses[eng][s]
        self.dma_last[eng][s] = o
        self.q[eng].append(o)
        return o

    def barrier_wait(self, eng, ops):
        o = Op(eng, None, False)
        for d in ops:
            o.deps.append(d)
            if not d.is_dma:
                d.flag = True
        self.q[eng].append(o)
        return o

    def emit(self, nc):
        for e in ENGS:
            c = 0
            for o in self.q[e]:
                if o.is_dma or o.fn is None:
                    continue
                if o.flag:
                    c += 1
                    o.semval = c
        dma_engs = [e for e in ENGS if any(o.is_dma for o in self.q[e])]
        with contextlib.ExitStack() as st:
            esem = {e: st.enter_context(nc.semaphore("s_" + e)) for e in ENGS}
            dsem = {(e, i): st.enter_context(nc.semaphore("d_%s_%d" % (e, i)))
                    for e in dma_engs for i in range(self.n_dma_sems)}
            block = st.enter_context(nc.Block())
            handles = {"pe": block.tensor, "act": block.scalar, "dve": block.vector,
                       "pool": block.gpsimd, "sp": block.sync}

            def run(e):
                ops = self.q[e]
                mysem = esem[e]

                def body(eng):
                    seen = {}
                    for o in ops:
                        for d in o.deps:
                            if d.is_dma:
                                key = ("d", d.dsem)
                                sem = dsem[d.dsem]
                            else:
                                key = ("e", d.eng)
                                sem = esem[d.eng]
                            if seen.get(key, 0) < d.semval:
                                eng.wait_ge(sem, d.semval)
                                seen[key] = d.semval
                        if o.fn is None:
                            continue
                        ins = o.fn(eng)
                        if o.is_dma:
                            ins.then_inc(dsem[o.dsem], 16)
                        elif o.flag:
                            ins.then_inc(mysem, 1)
                return body

            for e in ENGS:
                if self.q[e]:
                    handles[e](run(e))


def make_cfg(D=4096, F=11008, NBC=18, TILE=4, NSLOT=3, ffn_groups=None):
    KC = D // 128
    FC = F // 128
    NH = D // 128
    NKV = NH // 4
    cfg = dict(D=D, F=F, NBC=NBC, TILE=TILE, NSLOT=NSLOT, KC=KC, FC=FC, NH=NH, NKV=NKV,
               QKV=(NH + 2 * NKV) * 128)
    subs = []
    c = 0
    while c < FC:
        n = min(4, FC - c)
        subs.append((c, n))
        c += n
    if ffn_groups is None:
        ng = 3 if FC >= 12 else 2
        per = (len(subs) + ng - 1) // ng
        ffn_groups = [subs[i * per:(i + 1) * per] for i in range(ng)]
        ffn_groups = [g for g in ffn_groups if g]
    cfg["ffn_groups"] = ffn_groups
    cfg["GMAX"] = max(sum(n for _, n in g) for g in ffn_groups)
    off = {}
    p = 0

    def add(name, n):
        nonlocal p
        off[name] = (p, n)
        p += n
    add("mixg", 2 * KC)
    add("ffng", 2 * KC)
    add("pw1b", 2 * KC)
    add("dww", KC * CW)
    add("dwb", KC)
    add("lng", KC)
    add("lnb", KC)
    add("fdw", 2 * FC * FW)
    add("fdb", 2 * FC)
    add("qg", 1)
    add("kg", 1)
    cfg["voff"] = off
    cfg["NV"] = p
    return cfg


FULL = make_cfg()


def t5_bucket(dist):
    n = np.maximum(dist, 0)
    is_small = n < 16
    nf = np.maximum(n, 1).astype(np.float32)
    large = 16 + (np.log(nf / np.float32(16)) / np.float32(math.log(128 / 16)) * np.float32(16)).astype(np.int32)
    large = np.minimum(large, 31)
    return np.where(is_small, n, large)


def build_nc(cfg):
    D, F, NBC, TILE, NSLOT = cfg["D"], cfg["F"], cfg["NBC"], cfg["TILE"], cfg["NSLOT"]
    KC, FC, NH, NKV, QKV = cfg["KC"], cfg["FC"], cfg["NH"], cfg["NKV"], cfg["QKV"]
    GMAX = cfg["GMAX"]
    voff, NV = cfg["voff"], cfg["NV"]
    TT = TILE * 128
    HW = CW - 1
    UCW = HW + TT

    nc = bass.Bass("TRN2", target_bir_lowering=False)
    dt_in = lambda name, shape, dt=F32: nc.dram_tensor(name, shape, dt, kind="ExternalInput").ap()
    x_d = dt_in("x", [NBC * 128, D])
    pw1_d = dt_in("pw1_w", [D, 2 * D])
    pw2_d = dt_in("pw2_w", [D, D])
    wqkv_d = dt_in("wqkv", [D, QKV])
    wo_d = dt_in("wo", [D, D])
    win_d = [dt_in("win%d" % l, [D, 2 * F]) for l in range(2)]
    wout_d = [dt_in("wout%d" % l, [F, D]) for l in range(2)]
    vecs_d = dt_in("vecs", [128, NV])
    pw2b_d = dt_in("pw2b", [1, D])
    sinks_d = dt_in("sinks", [1, NH])
    biasT_d = dt_in("biasT", [128, NKV, 2, 512])
    ident_d = dt_in("ident", [128, 128], BF16)
    y_d = nc.dram_tensor("y", [NBC * 128, D], F32, kind="ExternalOutput").ap()

    with contextlib.ExitStack() as st:
        sb = lambda name, shape, dt: st.enter_context(nc.sbuf_tensor(name, shape, dt))
        UBW = max(KC * UCW, GMAX * TT, NH * TT, TILE * D)
        XR = sb("XR", [128, TILE, D], F32)
        HT = sb("HT", [128, KC, TT], BF16)
        UB = sb("UB", [128, UBW], BF16)
        WS = [sb("ws%d" % i, [128, 4096], BF16) for i in range(NSLOT)]
        VEC = sb("VEC", [128, NV], F32)
        IDT = sb("IDT", [128, 128], BF16)
        ONES = sb("ONES", [128, 128], BF16)
        SINKE = sb("SINKE", [128, NH], F32)
        CCu = sb("CCu", [128, KC, HW], BF16)
        GC = sb("GC", [128, 2 * FC, 2], F32)
        KCr = sb("KCr", [128, NKV, 128], BF16)
        VCr = sb("VCr", [128, NKV * 128], BF16)
        SS = sb("SS", [128, 8], F32)
        F32A = [sb("f32a%d" % i, [128, TT + 2], F32) for i in range(2)]
        F32Bt = sb("f32b", [128, 2, TT], F32)
        F32B = [F32Bt[:, 0, :], F32Bt[:, 1, :]]
        F32C = [sb("f32c%d" % i, [128, TT], F32) for i in range(2)]
        BF2 = [sb("bf2%d" % i, [128, 2, TT], BF16) for i in range(2)]
        QT = sb("QT", [128, 4, TT], BF16)
        NDG = 16
        DG = sb("DG", [128, NDG, 128], BF16)
        dgstate = {"i": 0}
        KTf = sb("KTf", [128, 2 * (128 + TT) // 1], F32)
        VTf = sb("VTf", [128, (TILE + 1) * 256], F32)
        KT = KTf[:].bitcast(BF16).rearrange("p (h t) -> p h t", t=128 + TT)
        VT = VTf[:].bitcast(BF16).rearrange("p (b n) -> p b n", n=512)
        MEAN = VTf[:, 0:TT]
        RSTD = VTf[:, TT:2 * TT]
        BBC = [KTf[:, 0:512], KTf[:, 512:1024]]
        ABI = F32Bt
        PS = [st.enter_context(nc.psum_tensor("ps%d" % i, [128, 512], F32)) for i in range(8)]

        P = Prog()
        B_XR = [Buf("xr%d" % b) for b in range(TILE)]
        B_HT = [Buf("ht%d" % c) for c in range(KC)]
        NUB = (UBW + 511) // 512
        B_UB = [Buf("ub%d" % i) for i in range(NUB)]
        B_WS = [Buf("ws%d" % i) for i in range(NSLOT)]
        B_PS = [Buf("ps%d" % i, excl=True) for i in range(8)]
        B_VEC, B_IDT, B_ONES, B_SINKE = Buf("vec"), Buf("idt"), Buf("ones"), Buf("sinke")
        B_CCu, B_GC, B_KCr, B_VCr = Buf("ccu"), Buf("gc"), Buf("kcr"), Buf("vcr")
        B_SS = Buf("ss")
        B_A = [Buf("a0"), Buf("a1")]
        B_B = [Buf("b0"), Buf("b1")]
        B_C = [Buf("c0"), Buf("c1")]
        B_BF2 = [Buf("bf20"), Buf("bf21")]
        B_QT, B_KT, B_VT = Buf("qt"), Buf("kt"), Buf("vt")
        B_DG = [Buf("dg%d" % i) for i in range(NDG)]
        B_MEAN = B_RSTD = B_VT
        B_BBC = [B_KT, B_KT]

        def ub_bufs(lo, hi):
            return B_UB[lo // 512:(hi + 511) // 512]

        UBc = UB[:, 0:KC * UCW].rearrange("p (c t) -> p c t", t=UCW)
        UBa = UB[:, 0:GMAX * TT].rearrange("p (c t) -> p c t", t=TT)
        UBo = UB[:, 0:NH * TT].rearrange("p (h t) -> p h t", t=TT)
        UBx = UB[:, 0:TILE * D].rearrange("p (b d) -> p b d", d=D)

        def vcol(name, i):
            o, n = voff[name]
            return VEC[:, o + i:o + i + 1]

        rot = {"a": 0, "b": 0, "c": 0, "bf2": 0, "bbc": 0}

        def nxt(k):
            rot[k] ^= 1
            return rot[k]

        wstate = {"i": 0}

        def wload(src_ap, nk, ncols):
            s = wstate["i"] % NSLOT
            wstate["i"] += 1
            dst = WS[s][:, 0:nk * ncols].rearrange("p (k n) -> p k n", n=ncols)
            P.dma("pool", lambda e: e.dma_start(out=dst, in_=src_ap), writes=[B_WS[s]])
            return dst, B_WS[s]

        def wtile(w_ap, r0, nk, c0, ncols):
            v = w_ap[r0 * 128:(r0 + nk) * 128, c0:c0 + ncols].rearrange("(k p) n -> p k n", p=128)
            return wload(v, nk, ncols)

        P.dma("sp", lambda e: e.dma_start(out=VEC[:], in_=vecs_d), writes=[B_VEC])
        P.dma("sp", lambda e: e.dma_start(out=IDT[:], in_=ident_d), writes=[B_IDT])
        dbg = cfg.get("dbg", 0)
        P.op("dve", lambda e: e.memset(ONES[:], 1.0), writes=[B_ONES])
        if not dbg & 1:
            P.dma("sp", lambda e: e.dma_start(out=SINKE[:], in_=sinks_d.partition_broadcast(128)), writes=[B_SINKE])
            P.op("act", lambda e: e.activation(out=SINKE[:], in_=SINKE[:], func=AF.Exp), reads=[B_SINKE], writes=[B_SINKE])
        if not dbg & 2:
            P.op("dve", lambda e: e.memset(CCu[:], 0.0), writes=[B_CCu])
            P.op("dve", lambda e: e.memset(GC[:], 0.0), writes=[B_GC])
            P.op("dve", lambda e: e.memset(KCr[:], 0.0), writes=[B_KCr])
            P.op("dve", lambda e: e.memset(VCr[:], 0.0), writes=[B_VCr])

        B_ABI = [B_B[0], B_B[1]]

        def ktiles(nchunks, ncols):
            nkmax = max(1, 4096 // ncols)
            out = []
            k = 0
            while k < nchunks:
                n = min(nkmax, nchunks - k)
                out.append((k, n))
                k += n
            return out

        def fm_matmul(w_ap, c0, nch, banks, T):
            for (k0, nk) in ktiles(KC, nch * 128):
                wt, wb = wtile(w_ap, k0, nk, c0, nch * 128)
                for i in range(nch):
                    for k in range(nk):
                        kk = k0 + k
                        P.op("pe", lambda e, o=PS[banks[i]][:, 0:T], l=wt[:, k, i * 128:(i + 1) * 128],
                             r=HT[:, kk, 0:T], st_=(kk == 0), sp_=(kk == KC - 1): e.matmul(o, l, r, start=st_, stop=sp_),
                             reads=[wb, B_HT[kk]], writes=[B_PS[banks[i]]])

        def tm_matmul(w_ap, r0, kchunks, lhs_fn, lhs_bufs_fn, nb, bankset, c0, ncols):
            for (k0, nk) in ktiles(kchunks, ncols):
                wt, wb = wtile(w_ap, r0 + k0, nk, c0, ncols)
                for b in range(nb):
                    for k in range(nk):
                        kk = k0 + k
                        P.op("pe", lambda e, o=PS[bankset[b]][:, 0:ncols], l=lhs_fn(kk, b), r=wt[:, k, :],
                             st_=(kk == 0), sp_=(kk == kchunks - 1): e.matmul(o, l, r, start=st_, stop=sp_),
                             reads=[wb] + lhs_bufs_fn(kk), writes=[B_PS[bankset[b]]])

        def rmsnorm_to_HT(nb, gname, layer):
            for b in range(nb):
                junk = UBx[:, b, :]
                jb = ub_bufs(b * D, (b + 1) * D)
                P.op("act", lambda e, o=junk, i=XR[:, b, :], a=SS[:, b:b + 1]: e.activation(
                    out=o, in_=i, func=AF.Square, accum_out=a), reads=[B_XR[b]], writes=jb + [B_SS])
                P.op("act", lambda e, o=SS[:, 4 + b:5 + b], i=SS[:, b:b + 1]: e.activation(
                    out=o, in_=i, func=AF.Sqrt, scale=1.0 / D, bias=EPS), reads=[B_SS], writes=[B_SS])
                P.op("dve", lambda e, o=SS[:, 4 + b:5 + b]: e.reciprocal(o, o), reads=[B_SS], writes=[B_SS])
                P.op("dve", lambda e, o=junk, i=XR[:, b, :], r=SS[:, 4 + b:5 + b]: e.tensor_scalar(o, i, r, None, ALU.mult),
                     reads=[B_XR[b], B_SS], writes=jb)
                if cfg.get("dbg", 0) & 4:
                    continue
                for c0 in range(0, KC, 8):
                    ncb = min(8, KC - c0)
                    bank = 4 + ((c0 // 8) % 4)
                    pv = PS[bank][:].bitcast(BF16)
                    for c in range(ncb):
                        P.op("pe", lambda e, o=pv[:, c * 128:(c + 1) * 128], i=junk[:, (c0 + c) * 128:(c0 + c + 1) * 128]:
                             e.transpose(o, i, IDT[:]), reads=jb + [B_IDT], writes=[B_PS[bank]])
                    for c in range(ncb):
                        if cfg.get("dbg", 0) & 8:
                            continue
                        gcol = vcol(gname, layer * KC + c0 + c)
                        o = HT[:, c0 + c, b * 128:(b + 1) * 128]
                        i = pv[:, c * 128:(c + 1) * 128]
                        if ((c0 // 8) + b) % 2 == 0:
                            P.op("act", lambda e, o=o, i=i, g=gcol: e.activation(out=o, in_=i, func=AF.Copy, scale=g),
                                 reads=[B_PS[bank], B_VEC], writes=[B_HT[c0 + c]])
                        else:
                            P.op("dve", lambda e, o=o, i=i, g=gcol: e.tensor_scalar(o, i, g, None, ALU.mult),
                                 reads=[B_PS[bank], B_VEC], writes=[B_HT[c0 + c]])

        def resid_evac(nb, bankset, c0, ncols, bias_ap=None, bias_buf=None):
            for b in range(nb):
                dst = XR[:, b, c0:c0 + ncols]
                P.op("dve", lambda e, d=dst, p=PS[bankset[b]][:, 0:ncols]: e.tensor_tensor(d, p, d, ALU.add),
                     reads=[B_PS[bankset[b]], B_XR[b]], writes=[B_XR[b]])
                if bias_ap is not None:
                    P.op("dve", lambda e, d=dst, bi=bias_ap: e.tensor_tensor(d, d, bi, ALU.add),
                         reads=[B_XR[b], bias_buf], writes=[B_XR[b]])

        def tm_project(w_ap, r0, kchunks, lhs_fn, lhs_bufs_fn, nb, bias=False):
            ncolchunks = D // 512
            for j in range(ncolchunks):
                bankset = [0, 1, 2, 3] if j % 2 == 0 else [4, 5, 6, 7]
                bap = bbuf = None
                if bias:
                    q = nxt("bbc")
                    P.dma("sp", lambda e, o=BBC[q], i=pw2b_d[:, j * 512:(j + 1) * 512].partition_broadcast(128):
                          e.dma_start(out=o, in_=i), writes=[B_BBC[q]])
                    bap, bbuf = BBC[q], B_BBC[q]
                tm_matmul(w_ap, r0, kchunks, lhs_fn, lhs_bufs_fn, nb, bankset, j * 512, 512)
                resid_evac(nb, bankset, j * 512, 512, bap, bbuf)

        def conformer(nb):
            T = nb * 128
            sub = cfg.get("sub", 9)
            if sub >= 1:
                rmsnorm_to_HT(nb, "mixg", 0)
            if sub < 2:
                return
            P.op("act", lambda e: e.activation(out=UBc[:, :, 0:HW], in_=CCu[:, :, :], func=AF.Copy),
                 reads=[B_CCu], writes=ub_bufs(0, KC * UCW))
            ab, gb, cvb = [0, 1], [2, 3], [4, 5]
            groups = [(c0, min(2, KC - c0)) for c0 in range(0, KC, 2)]

            def glu_group(c0, nch):
                fm_matmul(pw1_d, c0 * 128, nch, ab, T)
                fm_matmul(pw1_d, D + c0 * 128, nch, gb, T)
                for i in range(nch):
                    c = c0 + i
                    a = nxt("a")
                    ubb = ub_bufs(c * UCW, (c + 1) * UCW)
                    P.op("act", lambda e, o=F32A[a][:, 0:T], i_=PS[gb[i]][:, 0:T], bi=vcol("pw1b", KC + c): e.activation(
                        out=o, in_=i_, func=AF.Sigmoid, bias=bi), reads=[B_PS[gb[i]], B_VEC], writes=[B_A[a]])
                    P.op("dve", lambda e, o=UBc[:, c, HW:HW + T], p=PS[ab[i]][:, 0:T], bi=vcol("pw1b", c), sg=F32A[a][:, 0:T]:
                         e.scalar_tensor_tensor(o, p, bi, sg, ALU.add, ALU.mult),
                         reads=[B_PS[ab[i]], B_A[a], B_VEC], writes=ubb)

            def conv_group(c0, nch):
                for i in range(nch):
                    c = c0 + i
                    ubb = ub_bufs(c * UCW, (c + 1) * UCW)
                    cb = cvb[i]
                    for k in range(CW):
                        sl = dgstate["i"] % NDG
                        dgstate["i"] += 1
                        P.op("dve", lambda e, o=DG[:, sl, :], w=vcol("dww", c * CW + k): e.tensor_scalar(o, IDT[:], w, None, ALU.mult),
                             reads=[B_IDT, B_VEC], writes=[B_DG[sl]])
                        P.op("pe", lambda e, o=PS[cb][:, 0:T], l=DG[:, sl, :], r=UBc[:, c, k:k + T], st_=(k == 0), sp_=(k == CW - 1):
                             e.matmul(o, l, r, start=st_, stop=sp_), reads=[B_DG[sl]] + ubb, writes=[B_PS[cb]])
                    P.op("act", lambda e, o=CCu[:, c, :], i_=UBc[:, c, T:T + HW]: e.activation(out=o, in_=i_, func=AF.Copy),
                         reads=ubb, writes=[B_CCu])
                    P.op("act", lambda e, o=UBc[:, c, HW:HW + T], i_=PS[cb][:, 0:T], bi=vcol("dwb", c): e.activation(
                        out=o, in_=i_, func=AF.Identity, bias=bi), reads=[B_PS[cb], B_VEC], writes=ubb)
                    q = nxt("bf2")
                    P.op("act", lambda e, o=BF2[q][:, 0, 0:T], i_=PS[cb][:, 0:T], bi=vcol("dwb", c): e.activation(
                        out=o, in_=i_, func=AF.Square, bias=bi), reads=[B_PS[cb], B_VEC], writes=[B_BF2[q]])
                    P.op("pe", lambda e, o=PS[6][:, 0:T], r=UBc[:, c, HW:HW + T], st_=(c == 0), sp_=(c == KC - 1):
                         e.matmul(o, ONES[:], r, start=st_, stop=sp_), reads=ubb + [B_ONES], writes=[B_PS[6]])
                    P.op("pe", lambda e, o=PS[7][:, 0:T], r=BF2[q][:, 0, 0:T], st_=(c == 0), sp_=(c == KC - 1):
                         e.matmul(o, ONES[:], r, start=st_, stop=sp_), reads=[B_BF2[q], B_ONES], writes=[B_PS[7]])

            for j in range(len(groups) + 1):
                if j < len(groups):
                    glu_group(*groups[j])
                if j >= 1:
                    conv_group(*groups[j - 1])
            if sub < 4:
                return
            c_ = nxt("c")
            msq = F32C[c_][:, 0:T]
            mean = MEAN[:, 0:T]
            rstd = RSTD[:, 0:T]
            P.op("dve", lambda e: e.tensor_scalar(mean, PS[6][:, 0:T], 1.0 / D, None, ALU.mult),
                 reads=[B_PS[6]], writes=[B_MEAN])
            P.op("dve", lambda e: e.tensor_tensor(msq, mean, mean, ALU.mult), reads=[B_MEAN], writes=[B_C[c_]])
            P.op("dve", lambda e: e.scalar_tensor_tensor(rstd, PS[7][:, 0:T], 1.0 / D, msq, ALU.mult, ALU.subtract),
                 reads=[B_PS[7], B_C[c_]], writes=[B_RSTD])
            P.op("act", lambda e: e.activation(out=rstd, in_=rstd, func=AF.Sqrt, bias=EPS), reads=[B_RSTD], writes=[B_RSTD])
            P.op("dve", lambda e: e.reciprocal(rstd, rstd), reads=[B_RSTD], writes=[B_RSTD])
            for c in range(KC):
                ubb = ub_bufs(c * UCW, (c + 1) * UCW)
                a = nxt("a")
                t = F32A[a][:, 0:T]
                P.op("dve", lambda e, t=t, v=UBc[:, c, HW:HW + T]: e.tensor_tensor(t, v, mean, ALU.subtract),
                     reads=ubb + [B_MEAN], writes=[B_A[a]])
                P.op("dve", lambda e, t=t: e.tensor_tensor(t, t, rstd, ALU.mult), reads=[B_A[a], B_RSTD], writes=[B_A[a]])
                P.op("act", lambda e, t=t, o=HT[:, c, 0:T], g=vcol("lng", c), bi=vcol("lnb", c): e.activation(
                    out=o, in_=t, func=AF.Silu, scale=g, bias=bi), reads=[B_A[a], B_VEC], writes=[B_HT[c]])
            if sub < 5:
                return
            tm_project(pw2_d, 0, KC, lambda kk, b: HT[:, kk, b * 128:(b + 1) * 128], lambda kk: [B_HT[kk]], nb, bias=True)

        def ffn(nb, l):
            T = nb * 128
            rmsnorm_to_HT(nb, "ffng", l)
            for grp in cfg["ffn_groups"]:
                gch0 = grp[0][0]
                gn = sum(n for _, n in grp)
                for (c0, nch) in grp:
                    fm_matmul(win_d[l], c0 * 128, nch, [0, 1, 2, 3], T)
                    fm_matmul(win_d[l], F + c0 * 128, nch, [4, 5, 6, 7], T)
                    for i in range(nch):
                        c = c0 + i
                        cg = c - gch0
                        a = nxt("a")
                        G = F32A[a]
                        gcr = GC[:, l * FC + c, :]
                        P.op("act", lambda e, o=G[:, 0:2], i_=gcr: e.activation(out=o, in_=i_, func=AF.Copy),
                             reads=[B_GC], writes=[B_A[a]])
                        P.op("act", lambda e, o=G[:, 2:2 + T], i_=PS[i][:, 0:T]: e.activation(out=o, in_=i_, func=AF.Copy),
                             reads=[B_PS[i]], writes=[B_A[a]])
                        P.op("act", lambda e, o=gcr, i_=G[:, T:T + 2]: e.activation(out=o, in_=i_, func=AF.Copy),
                             reads=[B_A[a]], writes=[B_GC])
                        bb = nxt("b")
                        acc = F32B[bb][:, 0:T]
                        wi = (l * FC + c) * FW
                        P.op("dve", lambda e, o=acc, i_=G[:, 2:2 + T], w=vcol("fdw", wi + 2), bi=vcol("fdb", l * FC + c):
                             e.tensor_scalar(o, i_, w, bi, ALU.mult, ALU.add), reads=[B_A[a], B_VEC], writes=[B_B[bb]])
                        P.op("dve", lambda e, o=acc, i_=G[:, 1:1 + T], w=vcol("fdw", wi + 1):
                             e.scalar_tensor_tensor(o, i_, w, o, ALU.mult, ALU.add),
                             reads=[B_A[a], B_VEC, B_B[bb]], writes=[B_B[bb]])
                        P.op("dve", lambda e, o=acc, i_=G[:, 0:T], w=vcol("fdw", wi + 0):
                             e.scalar_tensor_tensor(o, i_, w, o, ALU.mult, ALU.add),
                             reads=[B_A[a], B_VEC, B_B[bb]], writes=[B_B[bb]])
                        cc = nxt("c")
                        S = F32C[cc][:, 0:T]
                        P.op("act", lambda e, o=S, i_=acc: e.activation(out=o, in_=i_, func=AF.Silu),
                             reads=[B_B[bb]], writes=[B_C[cc]])
                        P.op("dve", lambda e, o=UBa[:, cg, 0:T], s_=S, v=PS[4 + i][:, 0:T]: e.tensor_tensor(o, s_, v, ALU.mult),
                             reads=[B_C[cc], B_PS[4 + i]], writes=ub_bufs(cg * TT, (cg + 1) * TT))
                tm_project(wout_d[l], gch0, gn, lambda kk, b: UBa[:, kk, b * 128:(b + 1) * 128],
                           lambda kk: ub_bufs(kk * TT, (kk + 1) * TT), nb)

        def qk_norm(bank, T, gname, dst_ap, dst_bufs, ssbank):
            q = nxt("bf2")
            sq = BF2[q][:, 0, 0:T]
            P.op("act", lambda e: e.activation(out=sq, in_=PS[bank][:, 0:T], func=AF.Square),
                 reads=[B_PS[bank]], writes=[B_BF2[q]])
            P.op("pe", lambda e: e.matmul(PS[ssbank][:, 0:T], ONES[:], sq, start=True, stop=True),
                 reads=[B_BF2[q], B_ONES], writes=[B_PS[ssbank]])
            cc = nxt("c")
            r = F32C[cc][:, 0:T]
            P.op("act", lambda e: e.activation(out=r, in_=PS[ssbank][:, 0:T], func=AF.Sqrt, scale=1.0 / 128, bias=EPS),
                 reads=[B_PS[ssbank]], writes=[B_C[cc]])
            P.op("dve", lambda e: e.reciprocal(r, r), reads=[B_C[cc]], writes=[B_C[cc]])
            P.op("dve", lambda e: e.scalar_tensor_tensor(dst_ap, PS[bank][:, 0:T], vcol(gname, 0), r, ALU.mult, ALU.mult),
                 reads=[B_PS[bank], B_C[cc], B_VEC], writes=dst_bufs)

        def attention(nb, first_tile):
            T = nb * 128
            scale = 1.0 / math.sqrt(128.0)
            rmsnorm_to_HT(nb, "mixg", 1)
            for r0 in range(0, NKV, 4):
                nkv = min(4, NKV - r0)
                P.op("act", lambda e, o=KT[:, 0:nkv, 0:128], i_=KCr[:, r0:r0 + nkv, :]: e.activation(out=o, in_=i_, func=AF.Copy),
                     reads=[B_KCr], writes=[B_KT])
                fm_matmul(wqkv_d, NH * 128 + r0 * 128, nkv, [0, 1, 2, 3], T)
                for i in range(nkv):
                    qk_norm(i, T, "kg", KT[:, i, 128:128 + T], [B_KT], 7)
                P.op("act", lambda e, o=KCr[:, r0:r0 + nkv, :], i_=KT[:, 0:nkv, T:T + 128]: e.activation(out=o, in_=i_, func=AF.Copy),
                     reads=[B_KT], writes=[B_KCr])
                P.op("act", lambda e, o=VT[:, 0, 0:nkv * 128], i_=VCr[:, r0 * 128:(r0 + nkv) * 128]: e.activation(out=o, in_=i_, func=AF.Copy),
                     reads=[B_VCr], writes=[B_VT])
                tm_matmul(wqkv_d, 0, KC, lambda kk, b: HT[:, kk, b * 128:(b + 1) * 128], lambda kk: [B_HT[kk]],
                          nb, [0, 1, 2, 3], (NH + NKV) * 128 + r0 * 128, nkv * 128)
                for b in range(nb):
                    P.op("act", lambda e, o=VT[:, 1 + b, 0:nkv * 128], i_=PS[b][:, 0:nkv * 128]: e.activation(out=o, in_=i_, func=AF.Copy),
                         reads=[B_PS[b]], writes=[B_VT])
                P.op("act", lambda e, o=VCr[:, r0 * 128:(r0 + nkv) * 128], i_=VT[:, nb, 0:nkv * 128]: e.activation(out=o, in_=i_, func=AF.Copy),
                     reads=[B_VT], writes=[B_VCr])
                for jj in range(nkv):
                    kv = r0 + jj
                    P.dma("sp", lambda e, i_=biasT_d[:, kv, :, :]: e.dma_start(out=ABI[:], in_=i_), writes=B_ABI)
                    fm_matmul(wqkv_d, kv * 512, 4, [0, 1, 2, 3], T)
                    for i in range(4):
                        qk_norm(i, T, "qg", QT[:, i, 0:T], [B_QT], 7)
                    for b in range(nb):
                        has_prev = not (first_tile and b == 0)
                        rhs = QT[:, :, b * 128:(b + 1) * 128]
                        q = nxt("bf2")
                        parts = ([0] if has_prev else []) + [1]
                        for w in parts:
                            ko = b * 128 if w == 0 else 128 + b * 128
                            sbank = 4 + w
                            P.op("pe", lambda e, o=PS[sbank][:, :], l=KT[:, jj, ko:ko + 128], r=rhs: e.matmul(o, l, r, start=True, stop=True),
                                 reads=[B_KT, B_QT], writes=[B_PS[sbank]])
                            a = nxt("a")
                            t = F32A[a][:, 0:512]
                            P.op("dve", lambda e, t=t, p=PS[sbank][:, :], bi=ABI[:, w, :]: e.scalar_tensor_tensor(
                                t, p, scale, bi, ALU.mult, ALU.add), reads=[B_PS[sbank]] + B_ABI, writes=[B_A[a]])
                            P.op("act", lambda e, t=t, o=BF2[q][:, w, :]: e.activation(out=o, in_=t, func=AF.Exp),
                                 reads=[B_A[a]], writes=[B_BF2[q]])
                        for wi_, w in enumerate(parts):
                            P.op("pe", lambda e, r=BF2[q][:, w, :], st_=(wi_ == 0), sp_=(wi_ == len(parts) - 1):
                                 e.matmul(PS[6][:, :], ONES[:], r, start=st_, stop=sp_),
                                 reads=[B_BF2[q], B_ONES], writes=[B_PS[6]])
                        for wi_, w in enumerate(parts):
                            vb = b if w == 0 else b + 1
                            P.op("pe", lambda e, l=VT[:, vb, jj * 128:(jj + 1) * 128], r=BF2[q][:, w, :],
                                 st_=(wi_ == 0), sp_=(wi_ == len(parts) - 1): e.matmul(PS[7][:, :], l, r, start=st_, stop=sp_),
                                 reads=[B_BF2[q], B_VT], writes=[B_PS[7]])
                        cc = nxt("c")
                        rc = F32C[cc][:, 0:512]
                        for i in range(4):
                            P.op("dve", lambda e, o=rc[:, i * 128:(i + 1) * 128], p=PS[6][:, i * 128:(i + 1) * 128],
                                 sk=SINKE[:, kv * 4 + i:kv * 4 + i + 1]: e.tensor_scalar(o, p, sk, None, ALU.add),
                                 reads=[B_PS[6], B_SINKE], writes=[B_C[cc]])
                        P.op("dve", lambda e, rc=rc: e.reciprocal(rc, rc), reads=[B_C[cc]], writes=[B_C[cc]])
                        dst = UBo[:, kv * 4:kv * 4 + 4, b * 128:(b + 1) * 128]
                        P.op("dve", lambda e, rc=rc, dst=dst: e.tensor_tensor(
                            dst, PS[7][:, :].rearrange("p (h t) -> p h t", t=128),
                            rc.rearrange("p (h t) -> p h t", t=128), ALU.mult),
                            reads=[B_PS[7], B_C[cc]], writes=ub_bufs(kv * 4 * TT, (kv * 4 + 4) * TT))
            tm_project(wo_d, 0, NH, lambda kk, b: UBo[:, kk, b * 128:(b + 1) * 128],
                       lambda kk: ub_bufs(kk * TT, (kk + 1) * TT), nb)

        stores = []
        blk = 0
        first = True
        stages = cfg.get("stages", 4)
        while blk < NBC:
            nb = min(TILE, NBC - blk)
            for b in range(nb):
                P.dma("sp", lambda e, o=XR[:, b, :], i_=x_d[(blk + b) * 128:(blk + b + 1) * 128, :]: e.dma_start(out=o, in_=i_),
                      writes=[B_XR[b]])
            if stages >= 1:
                conformer(nb)
            if stages >= 2:
                ffn(nb, 0)
            if stages >= 3:
                attention(nb, first)
            if stages >= 4:
                ffn(nb, 1)
            for b in range(nb):
                stores.append(P.dma("sp", lambda e, o=y_d[(blk + b) * 128:(blk + b + 1) * 128, :], i_=XR[:, b, :]:
                                    e.dma_start(out=o, in_=i_), reads=[B_XR[b]]))
            blk += nb
            first = False
        P.barrier_wait("sp", stores)
        P.emit(nc)
    return nc


def fm(v):
    v = np.asarray(v, dtype=np.float32)
    lead = v.shape[:-1]
    C = v.shape[-1] // 128
    a = v.reshape(lead + (C, 128))
    a = np.moveaxis(a, -1, 0)
    return a


def prep_shared(cfg, inp):
    D, F, KC, FC, NH, NKV = cfg["D"], cfg["F"], cfg["KC"], cfg["FC"], cfg["NH"], cfg["NKV"]
    NV, voff = cfg["NV"], cfg["voff"]
    vecs = np.zeros((128, NV), np.float32)

    def put(name, arr):
        o, n = voff[name]
        vecs[:, o:o + n] = arr.reshape(128, n)
    put("mixg", fm(inp["mix_norm_g"]))
    put("ffng", fm(inp["ffn_norm_g"]))
    put("pw1b", fm(inp["conv_pw1_b"][0]))
    put("dww", np.transpose(fm(inp["conv_dw_w"][0]), (0, 2, 1)))
    put("dwb", fm(inp["conv_dw_b"][0]))
    put("lng", fm(inp["conv_ln_g"][0]))
    put("lnb", fm(inp["conv_ln_b"][0]))
    put("fdw", np.transpose(fm(inp["ffn_dw_w"]), (0, 1, 3, 2)))
    put("fdb", fm(inp["ffn_dw_b"]))
    put("qg", np.asarray(inp["attn_q_norm_g"][0], np.float32).reshape(128, 1))
    put("kg", np.asarray(inp["attn_k_norm_g"][0], np.float32).reshape(128, 1))
    rb = np.asarray(inp["rel_bias"], np.float32)
    tbl = np.concatenate([rb, np.full((1, NH), NEG, np.float32)], axis=0)
    qi = np.arange(128)[None, :]
    kj = np.arange(128)[:, None]
    idx = np.zeros((2, 128, 128), np.int64)
    d_prev = qi - kj + 128
    d_cur = qi - kj
    idx[0] = np.where((d_prev >= 0) & (d_prev < 128), t5_bucket(d_prev), 32)
    idx[1] = np.where((d_cur >= 0) & (d_cur < 128), t5_bucket(d_cur), 32)
    g = tbl[idx]
    g = g.reshape(2, 128, 128, NKV, 4)
    biasT = np.ascontiguousarray(np.transpose(g, (1, 3, 0, 4, 2))).reshape(128, NKV, 2, 512)
    shared = {
        "pw1_w": np.ascontiguousarray(inp["conv_pw1_w"][0], dtype=np.float32),
        "pw2_w": np.ascontiguousarray(inp["conv_pw2_w"][0], dtype=np.float32),
        "wqkv": np.ascontiguousarray(inp["attn_w_qkv"][0], dtype=np.float32),
        "wo": np.ascontiguousarray(inp["attn_w_o"][0], dtype=np.float32),
        "win0": np.ascontiguousarray(inp["ffn_w_in"][0], dtype=np.float32),
        "win1": np.ascontiguousarray(inp["ffn_w_in"][1], dtype=np.float32),
        "wout0": np.ascontiguousarray(inp["ffn_w_out"][0], dtype=np.float32),
        "wout1": np.ascontiguousarray(inp["ffn_w_out"][1], dtype=np.float32),
        "vecs": vecs,
        "pw2b": np.ascontiguousarray(inp["conv_pw2_b"][0], dtype=np.float32).reshape(1, D),
        "sinks": np.ascontiguousarray(inp["attn_sinks"][0], dtype=np.float32).reshape(1, NH),
        "biasT": biasT.astype(np.float32),
        "ident": np.eye(128, dtype=np.float32).astype(ml_dtypes.bfloat16),
    }
    return shared


def core_starts(nblk_total, nbc, ncores):
    step = (nblk_total) // ncores
    return [min(step * c, nblk_total - nbc) for c in range(ncores)]


_NC_CACHE = {}


def run_cfg(cfg, inp, ncores):
    D, NBC = cfg["D"], cfg["NBC"]
    x = np.asarray(inp["x"], dtype=np.float32)
    S = x.shape[1]
    nblk = S // 128
    starts = core_starts(nblk, NBC, ncores)
    shared = prep_shared(cfg, inp)
    in_maps = []
    for c in range(ncores):
        m = dict(shared)
        m["x"] = np.ascontiguousarray(x[0, starts[c] * 128:(starts[c] + NBC) * 128, :])
        in_maps.append(m)
    key = (D, cfg["F"], NBC, cfg["TILE"], cfg["NSLOT"])
    if key not in _NC_CACHE:
        _NC_CACHE[key] = build_nc(cfg)
    nc = _NC_CACHE[key]
    res = run_bass_kernel_spmd(nc, in_maps, core_ids=list(range(ncores)))
    out = np.empty((1, S, D), np.float32)
    for c in range(ncores):
        y = res.results[c]["y"]
        lo = 0 if starts[c] == 0 else 2
        out[0, (starts[c] + lo) * 128:(starts[c] + NBC) * 128, :] = y[lo * 128:, :]
    return out


def kernel(**inputs):
    return run_cfg(FULL, inputs, 8)
```

```python
import contextlib
import math
import numpy as np
import ml_dtypes
import concourse.bass as bass
import concourse.mybir as mybir
from concourse.bass_utils import run_bass_kernel_spmd

F32 = mybir.dt.float32
BF16 = mybir.dt.bfloat16
AF = mybir.ActivationFunctionType
ALU = mybir.AluOpType

ENGS = ("pe", "act", "dve", "pool", "sp")
EPS = 1e-6
CW = 31
FW = 3
NEG = -30000.0


class Buf:
    __slots__ = ("name", "w", "r", "excl")

    def __init__(self, name, excl=False):
        self.name = name
        self.w = None
        self.r = {}
        self.excl = excl


class Op:
    __slots__ = ("eng", "fn", "deps", "flag", "semval", "dsem", "is_dma")

    def __init__(self, eng, fn, is_dma):
        self.eng = eng
        self.fn = fn
        self.deps = []
        self.flag = False
        self.semval = 0
        self.dsem = None
        self.is_dma = is_dma


class Prog:
    def __init__(self, n_dma_sems=16):
        self.q = {e: [] for e in ENGS}
        self.n_dma_sems = n_dma_sems
        self.dma_last = {e: [None] * n_dma_sems for e in ENGS}
        self.dma_uses = {e: [0] * n_dma_sems for e in ENGS}
        self.dma_rr = {e: 0 for e in ENGS}

    def _collect(self, op, reads, writes):
        deps = {}
        for b in reads:
            if b.w is not None:
                deps[id(b.w)] = b.w
            if b.excl:
                for k_, r in b.r.items():
                    if r.eng != op.eng:
                        deps[id(r)] = r
        for b in writes:
            if b.w is not None:
                deps[id(b.w)] = b.w
            for r in b.r.values():
                deps[id(r)] = r
        for d in deps.values():
            if d is op:
                continue
            if d.eng == "pe" and op.eng == "pe" and not d.is_dma and not op.is_dma:
                continue
            op.deps.append(d)
            if not d.is_dma:
                d.flag = True
        key = op.eng + ("_d" if op.is_dma else "")
        for b in reads:
            b.r[key] = op
        for b in writes:
            b.w = op
            b.r = {}

    def op(self, eng, fn, reads=(), writes=()):
        o = Op(eng, fn, False)
        self._collect(o, reads, writes)
        self.q[eng].append(o)
        return o

    def dma(self, eng, fn, reads=(), writes=()):
        o = Op(eng, fn, True)
        s = self.dma_rr[eng]
        self.dma_rr[eng] = (s + 1) % self.n_dma_sems
        prev = self.dma_last[eng][s]
        self._collect(o, reads, writes)
        if prev is not None and prev not in o.deps:
            o.deps.append(prev)
        self.dma_uses[eng][s] += 1
        o.dsem = (eng, s)
        o.semval = 16 * self.dma_uses[eng][s]
        self.dma_last[eng][s] = o
        self.q[eng].append(o)
        return o

    def barrier_wait(self, eng, ops):
        o = Op(eng, None, False)
        for d in ops:
            o.deps.append(d)
            if not d.is_dma:
                d.flag = True
        self.q[eng].append(o)
        return o

    def emit(self, nc):
        for e in ENGS:
            c = 0
            for o in self.q[e]:
                if o.is_dma or o.fn is None:
                    continue
                if o.flag:
                    c += 1
                    o.semval = c
        dma_engs = [e for e in ENGS if any(o.is_dma for o in self.q[e])]
        with contextlib.ExitStack() as st:
            esem = {e: st.enter_context(nc.semaphore("s_" + e)) for e in ENGS}
            dsem = {(e, i): st.enter_context(nc.semaphore("d_%s_%d" % (e, i)))
                    for e in dma_engs for i in range(self.n_dma_sems)}
            block = st.enter_context(nc.Block())
            handles = {"pe": block.tensor, "act": block.scalar, "dve": block.vector,
                       "pool": block.gpsimd, "sp": block.sync}

            def run(e):
                ops = self.q[e]
                mysem = esem[e]

                def body(eng):
                    seen = {}
                    for o in ops:
                        for d in o.deps:
                            if d.is_dma:
                                key = ("d", d.dsem)
                                sem = dsem[d.dsem]
                            else:
                                key = ("e", d.eng)
                                sem = esem[d.eng]
                            if seen.get(key, 0) < d.semval:
                                eng.wait_ge(sem, d.semval)
                                seen[key] = d.semval
                        if o.fn is None:
                            continue
                        ins = o.fn(eng)
                        if o.is_dma:
                            ins.then_inc(dsem[o.dsem], 16)
                        elif o.flag:
                            ins.then_inc(mysem, 1)
                return body

            for e in ENGS:
                if self.q[e]:
                    handles[e](run(e))


def make_cfg(D=4096, F=11008, NBC=18, TILE=4, NSLOT=3, ffn_groups=None):
    KC = D // 128
    FC = F // 128
    NH = D // 128
    NKV = NH // 4
    cfg = dict(D=D, F=F, NBC=NBC, TILE=TILE, NSLOT=NSLOT, KC=KC, FC=FC, NH=NH, NKV=NKV,
               QKV=(NH + 2 * NKV) * 128)
    subs = []
    c = 0
    while c < FC:
        n = min(4, FC - c)
        subs.append((c, n))
        c += n
    if ffn_groups is None:
        ng = 3 if FC >= 12 else 2
        per = (len(subs) + ng - 1) // ng
        ffn_groups = [subs[i * per:(i + 1) * per] for i in range(ng)]
        ffn_groups = [g for g in ffn_groups if g]
    cfg["ffn_groups"] = ffn_groups
    cfg["GMAX"] = max(sum(n for _, n in g) for g in ffn_groups)
    off = {}
    p = 0

    def add(name, n):
        nonlocal p
        off[name] = (p, n)
        p += n
    add("mixg", 2 * KC)
    add("ffng", 2 * KC)
    add("pw1b", 2 * KC)
    add("dww", KC * CW)
    add("dwb", KC)
    add("lng", KC)
    add("lnb", KC)
    add("fdw", 2 * FC * FW)
    add("fdb", 2 * FC)
    add("qg", 1)
    add("kg", 1)
    cfg["voff"] = off
    cfg["NV"] = p
    return cfg


FULL = make_cfg()


def t5_bucket(dist):
    n = np.maximum(dist, 0)
    is_small = n < 16
    nf = np.maximum(n, 1).astype(np.float32)
    large = 16 + (np.log(nf / np.float32(16)) / np.float32(math.log(128 / 16)) * np.float32(16)).astype(np.int32)
    large = np.minimum(large, 31)
    return np.where(is_small, n, large)


def build_nc(cfg):
    D, F, NBC, TILE, NSLOT = cfg["D"], cfg["F"], cfg["NBC"], cfg["TILE"], cfg["NSLOT"]
    KC, FC, NH, NKV, QKV = cfg["KC"], cfg["FC"], cfg["NH"], cfg["NKV"], cfg["QKV"]
    GMAX = cfg["GMAX"]
    voff, NV = cfg["voff"], cfg["NV"]
    TT = TILE * 128
    HW = CW - 1
    UCW = HW + TT

    nc = bass.Bass("TRN2", target_bir_lowering=False)
    dt_in = lambda name, shape, dt=F32: nc.dram_tensor(name, shape, dt, kind="ExternalInput").ap()
    x_d = dt_in("x", [NBC * 128, D])
    pw1_d = dt_in("pw1_w", [D, 2 * D])
    pw2_d = dt_in("pw2_w", [D, D])
    wqkv_d = dt_in("wqkv", [D, QKV])
    wo_d = dt_in("wo", [D, D])
    win_d = [dt_in("win%d" % l, [D, 2 * F]) for l in range(2)]
    wout_d = [dt_in("wout%d" % l, [F, D]) for l in range(2)]
    vecs_d = dt_in("vecs", [128, NV])
    pw2b_d = dt_in("pw2b", [1, D])
    sinks_d = dt_in("sinks", [1, NH])
    biasT_d = dt_in("biasT", [128, NKV, 2, 512])
    ident_d = dt_in("ident", [128, 128], BF16)
    y_d = nc.dram_tensor("y", [NBC * 128, D], F32, kind="ExternalOutput").ap()

    with contextlib.ExitStack() as st:
        sb = lambda name, shape, dt: st.enter_context(nc.sbuf_tensor(name, shape, dt))
        UBW = max(KC * UCW, GMAX * TT, NH * TT, TILE * D)
        XR = sb("XR", [128, TILE, D], F32)
        HT = sb("HT", [128, KC, TT], BF16)
        UB = sb("UB", [128, UBW], BF16)
        WS = [sb("ws%d" % i, [128, 4096], BF16) for i in range(NSLOT)]
        VEC = sb("VEC", [128, NV], F32)
        IDT = sb("IDT", [128, 128], BF16)
        ONES = sb("ONES", [128, 128], BF16)
        SINKE = sb("SINKE", [128, NH], F32)
        CCu = sb("CCu", [128, KC, HW], BF16)
        GC = sb("GC", [128, 2 * FC, 2], F32)
        KCr = sb("KCr", [128, NKV, 128], BF16)
        VCr = sb("VCr", [128, NKV * 128], BF16)
        SS = sb("SS", [128, 8], F32)
        F32A = [sb("f32a%d" % i, [128, TT + 2], F32) for i in range(2)]
        F32Bt = sb("f32b", [128, 2, TT], F32)
        F32B = [F32Bt[:, 0, :], F32Bt[:, 1, :]]
        F32C = [sb("f32c%d" % i, [128, TT], F32) for i in range(2)]
        BF2 = [sb("bf2%d" % i, [128, 2, TT], BF16) for i in range(2)]
        QT = sb("QT", [128, 4, TT], BF16)
        NDG = 16
        DG = sb("DG", [128, NDG, 128], BF16)
        dgstate = {"i": 0}
        KTf = sb("KTf", [128, 2 * (128 + TT) // 1], F32)
        VTf = sb("VTf", [128, (TILE + 1) * 256], F32)
        KT = KTf[:].bitcast(BF16).rearrange("p (h t) -> p h t", t=128 + TT)
        VT = VTf[:].bitcast(BF16).rearrange("p (b n) -> p b n", n=512)
        MEAN = VTf[:, 0:TT]
        RSTD = VTf[:, TT:2 * TT]
        BBC = [KTf[:, 0:512], KTf[:, 512:1024]]
        ABI = F32Bt
        PS = [st.enter_context(nc.psum_tensor("ps%d" % i, [128, 512], F32)) for i in range(8)]

        P = Prog()
        B_XR = [Buf("xr%d" % b) for b in range(TILE)]
        B_HT = [Buf("ht%d" % c) for c in range(KC)]
        NUB = (UBW + 511) // 512
        B_UB = [Buf("ub%d" % i) for i in range(NUB)]
        B_WS = [Buf("ws%d" % i) for i in range(NSLOT)]
        B_PS = [Buf("ps%d" % i, excl=True) for i in range(8)]
        B_VEC, B_IDT, B_ONES, B_SINKE = Buf("vec"), Buf("idt"), Buf("ones"), Buf("sinke")
        B_CCu, B_GC, B_KCr, B_VCr = Buf("ccu"), Buf("gc"), Buf("kcr"), Buf("vcr")
        B_SS = Buf("ss")
        B_A = [Buf("a0"), Buf("a1")]
        B_B = [Buf("b0"), Buf("b1")]
        B_C = [Buf("c0"), Buf("c1")]
        B_BF2 = [Buf("bf20"), Buf("bf21")]
        B_QT, B_KT, B_VT = Buf("qt"), Buf("kt"), Buf("vt")
        B_DG = [Buf("dg%d" % i) for i in range(NDG)]
        B_MEAN = B_RSTD = B_VT
        B_BBC = [B_KT, B_KT]

        def ub_bufs(lo, hi):
            return B_UB[lo // 512:(hi + 511) // 512]

        UBc = UB[:, 0:KC * UCW].rearrange("p (c t) -> p c t", t=UCW)
        UBa = UB[:, 0:GMAX * TT].rearrange("p (c t) -> p c t", t=TT)
        UBo = UB[:, 0:NH * TT].rearrange("p (h t) -> p h t", t=TT)
        UBx = UB[:, 0:TILE * D].rearrange("p (b d) -> p b d", d=D)

        def vcol(name, i):
            o, n = voff[name]
            return VEC[:, o + i:o + i + 1]

        rot = {"a": 0, "b": 0, "c": 0, "bf2": 0, "bbc": 0}

        def nxt(k):
            rot[k] ^= 1
            return rot[k]

        wstate = {"i": 0}

        def wload(src_ap, nk, ncols):
            s = wstate["i"] % NSLOT
            wstate["i"] += 1
            dst = WS[s][:, 0:nk * ncols].rearrange("p (k n) -> p k n", n=ncols)
            P.dma("pool", lambda e: e.dma_start(out=dst, in_=src_ap), writes=[B_WS[s]])
            return dst, B_WS[s]

        def wtile(w_ap, r0, nk, c0, ncols):
            v = w_ap[r0 * 128:(r0 + nk) * 128, c0:c0 + ncols].rearrange("(k p) n -> p k n", p=128)
            return wload(v, nk, ncols)

        P.dma("sp", lambda e: e.dma_start(out=VEC[:], in_=vecs_d), writes=[B_VEC])
        P.dma("sp", lambda e: e.dma_start(out=IDT[:], in_=ident_d), writes=[B_IDT])
        dbg = cfg.get("dbg", 0)
        P.op("dve", lambda e: e.memset(ONES[:], 1.0), writes=[B_ONES])
        if not dbg & 1:
            P.dma("sp", lambda e: e.dma_start(out=SINKE[:], in_=sinks_d.partition_broadcast(128)), writes=[B_SINKE])
            P.op("act", lambda e: e.activation(out=SINKE[:], in_=SINKE[:], func=AF.Exp), reads=[B_SINKE], writes=[B_SINKE])
        if not dbg & 2:
            P.op("dve", lambda e: e.memset(CCu[:], 0.0), writes=[B_CCu])
            P.op("dve", lambda e: e.memset(GC[:], 0.0), writes=[B_GC])
            P.op("dve", lambda e: e.memset(KCr[:], 0.0), writes=[B_KCr])
            P.op("dve", lambda e: e.memset(VCr[:], 0.0), writes=[B_VCr])

        B_ABI = [B_B[0], B_B[1]]

        def ktiles(nchunks, ncols):
            nkmax = max(1, 4096 // ncols)
            out = []
            k = 0
            while k < nchunks:
                n = min(nkmax, nchunks - k)
                out.append((k, n))
                k += n
            return out

        def fm_pieces(w_ap, c0, nch, banks, T):
            pieces = []
            for (k0, nk) in ktiles(KC, nch * 128):
                def piece(k0=k0, nk=nk):
                    wt, wb = wtile(w_ap, k0, nk, c0, nch * 128)
                    for i in range(nch):
                        for k in range(nk):
                            kk = k0 + k
                            P.op("pe", lambda e, o=PS[banks[i]][:, 0:T], l=wt[:, k, i * 128:(i + 1) * 128],
                                 r=HT[:, kk, 0:T], st_=(kk == 0), sp_=(kk == KC - 1): e.matmul(o, l, r, start=st_, stop=sp_),
                                 reads=[wb, B_HT[kk]], writes=[B_PS[banks[i]]])
                pieces.append(piece)
            return pieces

        def fm_matmul(w_ap, c0, nch, banks, T):
            for p in fm_pieces(w_ap, c0, nch, banks, T):
                p()

        def tm_matmul(w_ap, r0, kchunks, lhs_fn, lhs_bufs_fn, nb, bankset, c0, ncols):
            for (k0, nk) in ktiles(kchunks, ncols):
                wt, wb = wtile(w_ap, r0 + k0, nk, c0, ncols)
                for b in range(nb):
                    for k in range(nk):
                        kk = k0 + k
                        P.op("pe", lambda e, o=PS[bankset[b]][:, 0:ncols], l=lhs_fn(kk, b), r=wt[:, k, :],
                             st_=(kk == 0), sp_=(kk == kchunks - 1): e.matmul(o, l, r, start=st_, stop=sp_),
                             reads=[wb] + lhs_bufs_fn(kk), writes=[B_PS[bankset[b]]])

        def rmsnorm_to_HT(nb, gname, layer):
            for b in range(nb):
                junk = UBx[:, b, :]
                jb = ub_bufs(b * D, (b + 1) * D)
                P.op("act", lambda e, o=junk, i=XR[:, b, :], a=SS[:, b:b + 1]: e.activation(
                    out=o, in_=i, func=AF.Square, accum_out=a), reads=[B_XR[b]], writes=jb + [B_SS])
                P.op("act", lambda e, o=SS[:, 4 + b:5 + b], i=SS[:, b:b + 1]: e.activation(
                    out=o, in_=i, func=AF.Sqrt, scale=1.0 / D, bias=EPS), reads=[B_SS], writes=[B_SS])
                P.op("dve", lambda e, o=SS[:, 4 + b:5 + b]: e.reciprocal(o, o), reads=[B_SS], writes=[B_SS])
                P.op("dve", lambda e, o=junk, i=XR[:, b, :], r=SS[:, 4 + b:5 + b]: e.tensor_scalar(o, i, r, None, ALU.mult),
                     reads=[B_XR[b], B_SS], writes=jb)
                if cfg.get("dbg", 0) & 4:
                    continue
                for c0 in range(0, KC, 8):
                    ncb = min(8, KC - c0)
                    bank = 4 + ((c0 // 8) % 4)
                    pv = PS[bank][:].bitcast(BF16)
                    for c in range(ncb):
                        P.op("pe", lambda e, o=pv[:, c * 128:(c + 1) * 128], i=junk[:, (c0 + c) * 128:(c0 + c + 1) * 128]:
                             e.transpose(o, i, IDT[:]), reads=jb + [B_IDT], writes=[B_PS[bank]])
                    for c in range(ncb):
                        if cfg.get("dbg", 0) & 8:
                            continue
                        gcol = vcol(gname, layer * KC + c0 + c)
                        o = HT[:, c0 + c, b * 128:(b + 1) * 128]
                        i = pv[:, c * 128:(c + 1) * 128]
                        if ((c0 // 8) + b) % 2 == 0:
                            P.op("act", lambda e, o=o, i=i, g=gcol: e.activation(out=o, in_=i, func=AF.Copy, scale=g),
                                 reads=[B_PS[bank], B_VEC], writes=[B_HT[c0 + c]])
                        else:
                            P.op("dve", lambda e, o=o, i=i, g=gcol: e.tensor_scalar(o, i, g, None, ALU.mult),
                                 reads=[B_PS[bank], B_VEC], writes=[B_HT[c0 + c]])

        def resid_evac(nb, bankset, c0, ncols, bias_ap=None, bias_buf=None):
            for b in range(nb):
                dst = XR[:, b, c0:c0 + ncols]
                P.op("dve", lambda e, d=dst, p=PS[bankset[b]][:, 0:ncols]: e.tensor_tensor(d, p, d, ALU.add),
                     reads=[B_PS[bankset[b]], B_XR[b]], writes=[B_XR[b]])
                if bias_ap is not None:
                    P.op("dve", lambda e, d=dst, bi=bias_ap: e.tensor_tensor(d, d, bi, ALU.add),
                         reads=[B_XR[b], bias_buf], writes=[B_XR[b]])

        def tm_project(w_ap, r0, kchunks, lhs_fn, lhs_bufs_fn, nb, bias=False):
            ncolchunks = D // 512
            for j in range(ncolchunks):
                bankset = [0, 1, 2, 3] if j % 2 == 0 else [4, 5, 6, 7]
                bap = bbuf = None
                if bias:
                    q = nxt("bbc")
                    P.dma("sp", lambda e, o=BBC[q], i=pw2b_d[:, j * 512:(j + 1) * 512].partition_broadcast(128):
                          e.dma_start(out=o, in_=i), writes=[B_BBC[q]])
                    bap, bbuf = BBC[q], B_BBC[q]
                tm_matmul(w_ap, r0, kchunks, lhs_fn, lhs_bufs_fn, nb, bankset, j * 512, 512)
                resid_evac(nb, bankset, j * 512, 512, bap, bbuf)

        def conformer(nb):
            T = nb * 128
            sub = cfg.get("sub", 9)
            if sub >= 1:
                rmsnorm_to_HT(nb, "mixg", 0)
            if sub < 2:
                return
            P.op("act", lambda e: e.activation(out=UBc[:, :, 0:HW], in_=CCu[:, :, :], func=AF.Copy),
                 reads=[B_CCu], writes=ub_bufs(0, KC * UCW))
            ab, gb, cvb = [0, 1], [2, 3], [4, 5]
            groups = [(c0, min(2, KC - c0)) for c0 in range(0, KC, 2)]

            def glu_group(c0, nch):
                fm_matmul(pw1_d, c0 * 128, nch, ab, T)
                fm_matmul(pw1_d, D + c0 * 128, nch, gb, T)
                for i in range(nch):
                    c = c0 + i
                    a = nxt("a")
                    ubb = ub_bufs(c * UCW, (c + 1) * UCW)
                    P.op("act", lambda e, o=F32A[a][:, 0:T], i_=PS[gb[i]][:, 0:T], bi=vcol("pw1b", KC + c): e.activation(
                        out=o, in_=i_, func=AF.Sigmoid, bias=bi), reads=[B_PS[gb[i]], B_VEC], writes=[B_A[a]])
                    P.op("dve", lambda e, o=UBc[:, c, HW:HW + T], p=PS[ab[i]][:, 0:T], bi=vcol("pw1b", c), sg=F32A[a][:, 0:T]:
                         e.scalar_tensor_tensor(o, p, bi, sg, ALU.add, ALU.mult),
                         reads=[B_PS[ab[i]], B_A[a], B_VEC], writes=ubb)

            def conv_group(c0, nch):
                for i in range(nch):
                    c = c0 + i
                    ubb = ub_bufs(c * UCW, (c + 1) * UCW)
                    cb = cvb[i]
                    for k in range(CW):
                        sl = dgstate["i"] % NDG
                        dgstate["i"] += 1
                        P.op("dve", lambda e, o=DG[:, sl, :], w=vcol("dww", c * CW + k): e.tensor_scalar(o, IDT[:], w, None, ALU.mult),
                             reads=[B_IDT, B_VEC], writes=[B_DG[sl]])
                        P.op("pe", lambda e, o=PS[cb][:, 0:T], l=DG[:, sl, :], r=UBc[:, c, k:k + T], st_=(k == 0), sp_=(k == CW - 1):
                             e.matmul(o, l, r, start=st_, stop=sp_), reads=[B_DG[sl]] + ubb, writes=[B_PS[cb]])
                    P.op("act", lambda e, o=CCu[:, c, :], i_=UBc[:, c, T:T + HW]: e.activation(out=o, in_=i_, func=AF.Copy),
                         reads=ubb, writes=[B_CCu])
                    P.op("act", lambda e, o=UBc[:, c, HW:HW + T], i_=PS[cb][:, 0:T], bi=vcol("dwb", c): e.activation(
                        out=o, in_=i_, func=AF.Identity, bias=bi), reads=[B_PS[cb], B_VEC], writes=ubb)
                    q = nxt("bf2")
                    P.op("act", lambda e, o=BF2[q][:, 0, 0:T], i_=PS[cb][:, 0:T], bi=vcol("dwb", c): e.activation(
                        out=o, in_=i_, func=AF.Square, bias=bi), reads=[B_PS[cb], B_VEC], writes=[B_BF2[q]])
                    P.op("pe", lambda e, o=PS[6][:, 0:T], r=UBc[:, c, HW:HW + T], st_=(c == 0), sp_=(c == KC - 1):
                         e.matmul(o, ONES[:], r, start=st_, stop=sp_), reads=ubb + [B_ONES], writes=[B_PS[6]])
                    P.op("pe", lambda e, o=PS[7][:, 0:T], r=BF2[q][:, 0, 0:T], st_=(c == 0), sp_=(c == KC - 1):
                         e.matmul(o, ONES[:], r, start=st_, stop=sp_), reads=[B_BF2[q], B_ONES], writes=[B_PS[7]])

            for j in range(len(groups) + 1):
                if j < len(groups):
                    glu_group(*groups[j])
                if j >= 1:
                    conv_group(*groups[j - 1])
            if sub < 4:
                return
            c_ = nxt("c")
            msq = F32C[c_][:, 0:T]
            mean = MEAN[:, 0:T]
            rstd = RSTD[:, 0:T]
            P.op("dve", lambda e: e.tensor_scalar(mean, PS[6][:, 0:T], 1.0 / D, None, ALU.mult),
                 reads=[B_PS[6]], writes=[B_MEAN])
            P.op("dve", lambda e: e.tensor_tensor(msq, mean, mean, ALU.mult), reads=[B_MEAN], writes=[B_C[c_]])
            P.op("dve", lambda e: e.scalar_tensor_tensor(rstd, PS[7][:, 0:T], 1.0 / D, msq, ALU.mult, ALU.subtract),
                 reads=[B_PS[7], B_C[c_]], writes=[B_RSTD])
            P.op("act", lambda e: e.activation(out=rstd, in_=rstd, func=AF.Sqrt, bias=EPS), reads=[B_RSTD], writes=[B_RSTD])
            P.op("dve", lambda e: e.reciprocal(rstd, rstd), reads=[B_RSTD], writes=[B_RSTD])
            for c in range(KC):
                ubb = ub_bufs(c * UCW, (c + 1) * UCW)
                a = nxt("a")
                t = F32A[a][:, 0:T]
                P.op("dve", lambda e, t=t, v=UBc[:, c, HW:HW + T]: e.tensor_tensor(t, v, mean, ALU.subtract),
                     reads=ubb + [B_MEAN], writes=[B_A[a]])
                P.op("dve", lambda e, t=t: e.tensor_tensor(t, t, rstd, ALU.mult), reads=[B_A[a], B_RSTD], writes=[B_A[a]])
                P.op("act", lambda e, t=t, o=HT[:, c, 0:T], g=vcol("lng", c), bi=vcol("lnb", c): e.activation(
                    out=o, in_=t, func=AF.Silu, scale=g, bias=bi), reads=[B_A[a], B_VEC], writes=[B_HT[c]])
            if sub < 5:
                return
            tm_project(pw2_d, 0, KC, lambda kk, b: HT[:, kk, b * 128:(b + 1) * 128], lambda kk: [B_HT[kk]], nb, bias=True)

        def ffn(nb, l):
            T = nb * 128
            rmsnorm_to_HT(nb, "ffng", l)
            for grp in cfg["ffn_groups"]:
                gch0 = grp[0][0]
                gn = sum(n for _, n in grp)
                for (c0, nch) in grp:
                    fm_matmul(win_d[l], c0 * 128, nch, [0, 1, 2, 3], T)
                    fm_matmul(win_d[l], F + c0 * 128, nch, [4, 5, 6, 7], T)
                    for i in range(nch):
                        c = c0 + i
                        cg = c - gch0
                        a = nxt("a")
                        G = F32A[a]
                        gcr = GC[:, l * FC + c, :]
                        P.op("act", lambda e, o=G[:, 0:2], i_=gcr: e.activation(out=o, in_=i_, func=AF.Copy),
                             reads=[B_GC], writes=[B_A[a]])
                        P.op("act", lambda e, o=G[:, 2:2 + T], i_=PS[i][:, 0:T]: e.activation(out=o, in_=i_, func=AF.Copy),
                             reads=[B_PS[i]], writes=[B_A[a]])
                        P.op("act", lambda e, o=gcr, i_=G[:, T:T + 2]: e.activation(out=o, in_=i_, func=AF.Copy),
                             reads=[B_A[a]], writes=[B_GC])
                        bb = nxt("b")
                        acc = F32B[bb][:, 0:T]
                        wi = (l * FC + c) * FW
                        P.op("dve", lambda e, o=acc, i_=G[:, 2:2 + T], w=vcol("fdw", wi + 2), bi=vcol("fdb", l * FC + c):
                             e.tensor_scalar(o, i_, w, bi, ALU.mult, ALU.add), reads=[B_A[a], B_VEC], writes=[B_B[bb]])
                        P.op("dve", lambda e, o=acc, i_=G[:, 1:1 + T], w=vcol("fdw", wi + 1):
                             e.scalar_tensor_tensor(o, i_, w, o, ALU.mult, ALU.add),
                             reads=[B_A[a], B_VEC, B_B[bb]], writes=[B_B[bb]])
                        P.op("dve", lambda e, o=acc, i_=G[:, 0:T], w=vcol("fdw", wi + 0):
                             e.scalar_tensor_tensor(o, i_, w, o, ALU.mult, ALU.add),
                             reads=[B_A[a], B_VEC, B_B[bb]], writes=[B_B[bb]])
                        cc = nxt("c")
                        S = F32C[cc][:, 0:T]
                        P.op("act", lambda e, o=S, i_=acc: e.activation(out=o, in_=i_, func=AF.Silu),
                             reads=[B_B[bb]], writes=[B_C[cc]])
                        P.op("dve", lambda e, o=UBa[:, cg, 0:T], s_=S, v=PS[4 + i][:, 0:T]: e.tensor_tensor(o, s_, v, ALU.mult),
                             reads=[B_C[cc], B_PS[4 + i]], writes=ub_bufs(cg * TT, (cg + 1) * TT))
                tm_project(wout_d[l], gch0, gn, lambda kk, b: UBa[:, kk, b * 128:(b + 1) * 128],
                           lambda kk: ub_bufs(kk * TT, (kk + 1) * TT), nb)

        def qk_norm(bank, T, gname, dst_ap, dst_bufs, ssbank):
            q = nxt("bf2")
            sq = BF2[q][:, 0, 0:T]
            P.op("act", lambda e: e.activation(out=sq, in_=PS[bank][:, 0:T], func=AF.Square),
                 reads=[B_PS[bank]], writes=[B_BF2[q]])
            P.op("pe", lambda e: e.matmul(PS[ssbank][:, 0:T], ONES[:], sq, start=True, stop=True),
                 reads=[B_BF2[q], B_ONES], writes=[B_PS[ssbank]])
            cc = nxt("c")
            r = F32C[cc][:, 0:T]
            P.op("act", lambda e: e.activation(out=r, in_=PS[ssbank][:, 0:T], func=AF.Sqrt, scale=1.0 / 128, bias=EPS),
                 reads=[B_PS[ssbank]], writes=[B_C[cc]])
            P.op("dve", lambda e: e.reciprocal(r, r), reads=[B_C[cc]], writes=[B_C[cc]])
            P.op("dve", lambda e: e.scalar_tensor_tensor(dst_ap, PS[bank][:, 0:T], vcol(gname, 0), r, ALU.mult, ALU.mult),
                 reads=[B_PS[bank], B_C[cc], B_VEC], writes=dst_bufs)

        def attention(nb, first_tile):
            T = nb * 128
            scale = 1.0 / math.sqrt(128.0)
            rmsnorm_to_HT(nb, "mixg", 1)
            QTs = [QT, DG[:].rearrange("p a b -> p (a b)").rearrange("p (h t) -> p h t", t=TT)]
            B_QTs = [[B_QT], list(B_DG)]
            rounds = [(r0, min(4, NKV - r0)) for r0 in range(0, NKV, 4)]

            def k_pieces(r0, nkv):
                return fm_pieces(wqkv_d, NH * 128 + r0 * 128, nkv, [0, 1, 2, 3], T)

            def k_finish(r0, nkv):
                P.op("act", lambda e, o=KT[:, 0:nkv, 0:128], i_=KCr[:, r0:r0 + nkv, :]: e.activation(out=o, in_=i_, func=AF.Copy),
                     reads=[B_KCr], writes=[B_KT])
                for i in range(nkv):
                    qk_norm(i, T, "kg", KT[:, i, 128:128 + T], [B_KT], 7)
                P.op("act", lambda e, o=KCr[:, r0:r0 + nkv, :], i_=KT[:, 0:nkv, T:T + 128]: e.activation(out=o, in_=i_, func=AF.Copy),
                     reads=[B_KT], writes=[B_KCr])

            def v_all(r0, nkv):
                P.op("act", lambda e, o=VT[:, 0, 0:nkv * 128], i_=VCr[:, r0 * 128:(r0 + nkv) * 128]: e.activation(out=o, in_=i_, func=AF.Copy),
                     reads=[B_VCr], writes=[B_VT])
                tm_matmul(wqkv_d, 0, KC, lambda kk, b: HT[:, kk, b * 128:(b + 1) * 128], lambda kk: [B_HT[kk]],
                          nb, [0, 1, 2, 3], (NH + NKV) * 128 + r0 * 128, nkv * 128)
                for b in range(nb):
                    P.op("act", lambda e, o=VT[:, 1 + b, 0:nkv * 128], i_=PS[b][:, 0:nkv * 128]: e.activation(out=o, in_=i_, func=AF.Copy),
                         reads=[B_PS[b]], writes=[B_VT])
                P.op("act", lambda e, o=VCr[:, r0 * 128:(r0 + nkv) * 128], i_=VT[:, nb, 0:nkv * 128]: e.activation(out=o, in_=i_, func=AF.Copy),
                     reads=[B_VT], writes=[B_VCr])

            def q_pieces(kv):
                return fm_pieces(wqkv_d, kv * 512, 4, [0, 1, 2, 3], T)

            def q_finish(kv):
                for i in range(4):
                    qk_norm(i, T, "qg", QTs[kv % 2][:, i, 0:T], B_QTs[kv % 2], 7)

            def attn_blocks(kv, jj, fillers):
                QTc, B_QTc = QTs[kv % 2], B_QTs[kv % 2]
                nf = len(fillers)
                per = [(nf + nb - 1 - b) // nb for b in range(nb)]
                for b in range(nb):
                    has_prev = not (first_tile and b == 0)
                    rhs = QTc[:, :, b * 128:(b + 1) * 128]
                    q = nxt("bf2")
                    parts = ([0] if has_prev else []) + [1]
                    for w in parts:
                        ko = b * 128 if w == 0 else 128 + b * 128
                        sbank = 4 + w
                        P.op("pe", lambda e, o=PS[sbank][:, :], l=KT[:, jj, ko:ko + 128], r=rhs: e.matmul(o, l, r, start=True, stop=True),
                             reads=[B_KT] + B_QTc, writes=[B_PS[sbank]])
                        a = nxt("a")
                        t = F32A[a][:, 0:512]
                        P.op("dve", lambda e, t=t, p=PS[sbank][:, :], bi=ABI[:, w, :]: e.scalar_tensor_tensor(
                            t, p, scale, bi, ALU.mult, ALU.add), reads=[B_PS[sbank]] + B_ABI, writes=[B_A[a]])
                        P.op("act", lambda e, t=t, o=BF2[q][:, w, :]: e.activation(out=o, in_=t, func=AF.Exp),
                             reads=[B_A[a]], writes=[B_BF2[q]])
                    for _ in range(per[b]):
                        fillers.pop(0)()
                    for wi_, w in enumerate(parts):
                        P.op("pe", lambda e, r=BF2[q][:, w, :], st_=(wi_ == 0), sp_=(wi_ == len(parts) - 1):
                             e.matmul(PS[6][:, :], ONES[:], r, start=st_, stop=sp_),
                             reads=[B_BF2[q], B_ONES], writes=[B_PS[6]])
                    for wi_, w in enumerate(parts):
                        vb = b if w == 0 else b + 1
                        P.op("pe", lambda e, l=VT[:, vb, jj * 128:(jj + 1) * 128], r=BF2[q][:, w, :],
                             st_=(wi_ == 0), sp_=(wi_ == len(parts) - 1): e.matmul(PS[7][:, :], l, r, start=st_, stop=sp_),
                             reads=[B_BF2[q], B_VT], writes=[B_PS[7]])
                    cc = nxt("c")
                    rc = F32C[cc][:, 0:512]
                    for i in range(4):
                        P.op("dve", lambda e, o=rc[:, i * 128:(i + 1) * 128], p=PS[6][:, i * 128:(i + 1) * 128],
                             sk=SINKE[:, kv * 4 + i:kv * 4 + i + 1]: e.tensor_scalar(o, p, sk, None, ALU.add),
                             reads=[B_PS[6], B_SINKE], writes=[B_C[cc]])
                    P.op("dve", lambda e, rc=rc: e.reciprocal(rc, rc), reads=[B_C[cc]], writes=[B_C[cc]])
                    dst = UBo[:, kv * 4:kv * 4 + 4, b * 128:(b + 1) * 128]
                    P.op("dve", lambda e, rc=rc, dst=dst: e.tensor_tensor(
                        dst, PS[7][:, :].rearrange("p (h t) -> p h t", t=128),
                        rc.rearrange("p (h t) -> p h t", t=128), ALU.mult),
                        reads=[B_PS[7], B_C[cc]], writes=ub_bufs(kv * 4 * TT, (kv * 4 + 4) * TT))
                while fillers:
                    fillers.pop(0)()

            for p in k_pieces(*rounds[0]):
                p()
            k_finish(*rounds[0])
            v_all(*rounds[0])
            for p in q_pieces(0):
                p()
            q_finish(0)
            for kv in range(NKV):
                ri, jj = divmod(kv, 4)
                P.dma("sp", lambda e, i_=biasT_d[:, kv, :, :]: e.dma_start(out=ABI[:], in_=i_), writes=B_ABI)
                fillers = []
                nxt_round = (kv + 1 < NKV) and ((kv + 1) % 4 == 0)
                if kv + 1 < NKV:
                    fillers = k_pieces(*rounds[ri + 1]) if nxt_round else q_pieces(kv + 1)
                attn_blocks(kv, jj, fillers)
                if kv + 1 < NKV:
                    if nxt_round:
                        k_finish(*rounds[ri + 1])
                        v_all(*rounds[ri + 1])
                        for p in q_pieces(kv + 1):
                            p()
                    q_finish(kv + 1)
            tm_project(wo_d, 0, NH, lambda kk, b: UBo[:, kk, b * 128:(b + 1) * 128],
                       lambda kk: ub_bufs(kk * TT, (kk + 1) * TT), nb)

        stores = []
        blk = 0
        first = True
        stages = cfg.get("stages", 4)
        ntiles = (NBC + TILE - 1) // TILE
        sched = [NBC // ntiles + (1 if i < NBC % ntiles else 0) for i in range(ntiles)]
        for nb in sched:
            for b in range(nb):
                P.dma("sp", lambda e, o=XR[:, b, :], i_=x_d[(blk + b) * 128:(blk + b + 1) * 128, :]: e.dma_start(out=o, in_=i_),
                      writes=[B_XR[b]])
            if stages >= 1:
                conformer(nb)
            if stages >= 2:
                ffn(nb, 0)
            if stages >= 3:
                attention(nb, first)
            if stages >= 4:
                ffn(nb, 1)
            for b in range(nb):
                stores.append(P.dma("sp", lambda e, o=y_d[(blk + b) * 128:(blk + b + 1) * 128, :], i_=XR[:, b, :]:
                                    e.dma_start(out=o, in_=i_), reads=[B_XR[b]]))
            blk += nb
            first = False
        P.barrier_wait("sp", stores)
        P.emit(nc)
    return nc


def fm(v):
    v = np.asarray(v, dtype=np.float32)
    lead = v.shape[:-1]
    C = v.shape[-1] // 128
    a = v.reshape(lead + (C, 128))
    a = np.moveaxis(a, -1, 0)
    return a


def prep_shared(cfg, inp):
    D, F, KC, FC, NH, NKV = cfg["D"], cfg["F"], cfg["KC"], cfg["FC"], cfg["NH"], cfg["NKV"]
    NV, voff = cfg["NV"], cfg["voff"]
    vecs = np.zeros((128, NV), np.float32)

    def put(name, arr):
        o, n = voff[name]
        vecs[:, o:o + n] = arr.reshape(128, n)
    put("mixg", fm(inp["mix_norm_g"]))
    put("ffng", fm(inp["ffn_norm_g"]))
    put("pw1b", fm(inp["conv_pw1_b"][0]))
    put("dww", np.transpose(fm(inp["conv_dw_w"][0]), (0, 2, 1)))
    put("dwb", fm(inp["conv_dw_b"][0]))
    put("lng", fm(inp["conv_ln_g"][0]))
    put("lnb", fm(inp["conv_ln_b"][0]))
    put("fdw", np.transpose(fm(inp["ffn_dw_w"]), (0, 1, 3, 2)))
    put("fdb", fm(inp["ffn_dw_b"]))
    put("qg", np.asarray(inp["attn_q_norm_g"][0], np.float32).reshape(128, 1))
    put("kg", np.asarray(inp["attn_k_norm_g"][0], np.float32).reshape(128, 1))
    rb = np.asarray(inp["rel_bias"], np.float32)
    tbl = np.concatenate([rb, np.full((1, NH), NEG, np.float32)], axis=0)
    qi = np.arange(128)[None, :]
    kj = np.arange(128)[:, None]
    idx = np.zeros((2, 128, 128), np.int64)
    d_prev = qi - kj + 128
    d_cur = qi - kj
    idx[0] = np.where((d_prev >= 0) & (d_prev < 128), t5_bucket(d_prev), 32)
    idx[1] = np.where((d_cur >= 0) & (d_cur < 128), t5_bucket(d_cur), 32)
    g = tbl[idx]
    g = g.reshape(2, 128, 128, NKV, 4)
    biasT = np.ascontiguousarray(np.transpose(g, (1, 3, 0, 4, 2))).reshape(128, NKV, 2, 512)
    shared = {
        "pw1_w": np.ascontiguousarray(inp["conv_pw1_w"][0], dtype=np.float32),
        "pw2_w": np.ascontiguousarray(inp["conv_pw2_w"][0], dtype=np.float32),
        "wqkv": np.ascontiguousarray(inp["attn_w_qkv"][0], dtype=np.float32),
        "wo": np.ascontiguousarray(inp["attn_w_o"][0], dtype=np.float32),
        "win0": np.ascontiguousarray(inp["ffn_w_in"][0], dtype=np.float32),
        "win1": np.ascontiguousarray(inp["ffn_w_in"][1], dtype=np.float32),
        "wout0": np.ascontiguousarray(inp["ffn_w_out"][0], dtype=np.float32),
        "wout1": np.ascontiguousarray(inp["ffn_w_out"][1], dtype=np.float32),
        "vecs": vecs,
        "pw2b": np.ascontiguousarray(inp["conv_pw2_b"][0], dtype=np.float32).reshape(1, D),
        "sinks": np.ascontiguousarray(inp["attn_sinks"][0], dtype=np.float32).reshape(1, NH),
        "biasT": biasT.astype(np.float32),
        "ident": np.eye(128, dtype=np.float32).astype(ml_dtypes.bfloat16),
    }
    return shared


def core_starts(nblk_total, nbc, ncores):
    step = (nblk_total) // ncores
    return [min(step * c, nblk_total - nbc) for c in range(ncores)]


_NC_CACHE = {}


def run_cfg(cfg, inp, ncores):
    D, NBC = cfg["D"], cfg["NBC"]
    x = np.asarray(inp["x"], dtype=np.float32)
    S = x.shape[1]
    nblk = S // 128
    starts = core_starts(nblk, NBC, ncores)
    shared = prep_shared(cfg, inp)
    in_maps = []
    for c in range(ncores):
        m = dict(shared)
        m["x"] = np.ascontiguousarray(x[0, starts[c] * 128:(starts[c] + NBC) * 128, :])
        in_maps.append(m)
    key = (D, cfg["F"], NBC, cfg["TILE"], cfg["NSLOT"])
    if key not in _NC_CACHE:
        _NC_CACHE[key] = build_nc(cfg)
    nc = _NC_CACHE[key]
    res = run_bass_kernel_spmd(nc, in_maps, core_ids=list(range(ncores)))
    out = np.empty((1, S, D), np.float32)
    for c in range(ncores):
        y = res.results[c]["y"]
        lo = 0 if starts[c] == 0 else 2
        out[0, (starts[c] + lo) * 128:(starts[c] + NBC) * 128, :] = y[lo * 128:, :]
    return out


def kernel(**inputs):
    return run_cfg(FULL, inputs, 8)
```

```python
import contextlib
import math
import numpy as np
import ml_dtypes
import concourse.bass as bass
import concourse.mybir as mybir
from concourse.bass_utils import run_bass_kernel_spmd

F32 = mybir.dt.float32
BF16 = mybir.dt.bfloat16
AF = mybir.ActivationFunctionType
ALU = mybir.AluOpType

ENGS = ("pe", "act", "dve", "pool", "sp")
EPS = 1e-6
CW = 31
FW = 3
NEG = -30000.0


class Buf:
    __slots__ = ("name", "w", "r", "excl")

    def __init__(self, name, excl=False):
        self.name = name
        self.w = None
        self.r = {}
        self.excl = excl


class Op:
    __slots__ = ("eng", "fn", "deps", "flag", "semval", "dsem", "is_dma")

    def __init__(self, eng, fn, is_dma):
        self.eng = eng
        self.fn = fn
        self.deps = []
        self.flag = False
        self.semval = 0
        self.dsem = None
        self.is_dma = is_dma


class Prog:
    def __init__(self, n_dma_sems=16):
        self.q = {e: [] for e in ENGS}
        self.n_dma_sems = n_dma_sems
        self.dma_last = {e: [None] * n_dma_sems for e in ENGS}
        self.dma_uses = {e: [0] * n_dma_sems for e in ENGS}
        self.dma_rr = {e: 0 for e in ENGS}

    def _collect(self, op, reads, writes):
        deps = {}
        for b in reads:
            if b.w is not None:
                deps[id(b.w)] = b.w
            if b.excl:
                for k_, r in b.r.items():
                    if r.eng != op.eng:
                        deps[id(r)] = r
        for b in writes:
            if b.w is not None:
                deps[id(b.w)] = b.w
            for r in b.r.values():
                deps[id(r)] = r
        for d in deps.values():
            if d is op:
                continue
            if d.eng == "pe" and op.eng == "pe" and not d.is_dma and not op.is_dma:
                continue
            op.deps.append(d)
            if not d.is_dma:
                d.flag = True
        key = op.eng + ("_d" if op.is_dma else "")
        for b in reads:
            b.r[key] = op
        for b in writes:
            b.w = op
            b.r = {}

    def op(self, eng, fn, reads=(), writes=()):
        o = Op(eng, fn, False)
        self._collect(o, reads, writes)
        self.q[eng].append(o)
        return o

    def dma(self, eng, fn, reads=(), writes=()):
        o = Op(eng, fn, True)
        s = self.dma_rr[eng]
        self.dma_rr[eng] = (s + 1) % self.n_dma_sems
        prev = self.dma_last[eng][s]
        self._collect(o, reads, writes)
        if prev is not None and prev not in o.deps:
            o.deps.append(prev)
        self.dma_uses[eng][s] += 1
        o.dsem = (eng, s)
        o.semval = 16 * self.dma_uses[eng][s]
        self.dma_last[eng][s] = o
        self.q[eng].append(o)
        return o

    def barrier_wait(self, eng, ops):
        o = Op(eng, None, False)
        for d in ops:
            o.deps.append(d)
            if not d.is_dma:
                d.flag = True
        self.q[eng].append(o)
        return o

    def emit(self, nc):
        for e in ENGS:
            c = 0
            for o in self.q[e]:
                if o.is_dma or o.fn is None:
                    continue
                if o.flag:
                    c += 1
                    o.semval = c
        dma_engs = [e for e in ENGS if any(o.is_dma for o in self.q[e])]
        with contextlib.ExitStack() as st:
            esem = {e: st.enter_context(nc.semaphore("s_" + e)) for e in ENGS}
            dsem = {(e, i): st.enter_context(nc.semaphore("d_%s_%d" % (e, i)))
                    for e in dma_engs for i in range(self.n_dma_sems)}
            block = st.enter_context(nc.Block())
            handles = {"pe": block.tensor, "act": block.scalar, "dve": block.vector,
                       "pool": block.gpsimd, "sp": block.sync}

            def run(e):
                ops = self.q[e]
                mysem = esem[e]

                def body(eng):
                    seen = {}
                    for o in ops:
                        for d in o.deps:
                            if d.is_dma:
                                key = ("d", d.dsem)
                                sem = dsem[d.dsem]
                            else:
                                key = ("e", d.eng)
                                sem = esem[d.eng]
                            if seen.get(key, 0) < d.semval:
                                eng.wait_ge(sem, d.semval)
                                seen[key] = d.semval
                        if o.fn is None:
                            continue
                        ins = o.fn(eng)
                        if o.is_dma:
                            ins.then_inc(dsem[o.dsem], 16)
                        elif o.flag:
                            ins.then_inc(mysem, 1)
                return body

            for e in ENGS:
                if self.q[e]:
                    handles[e](run(e))


def make_cfg(D=4096, F=11008, NBC=18, TILE=4, NSLOT=3, ffn_groups=None):
    KC = D // 128
    FC = F // 128
    NH = D // 128
    NKV = NH // 4
    cfg = dict(D=D, F=F, NBC=NBC, TILE=TILE, NSLOT=NSLOT, KC=KC, FC=FC, NH=NH, NKV=NKV,
               QKV=(NH + 2 * NKV) * 128)
    subs = []
    c = 0
    while c < FC:
        n = min(4, FC - c)
        subs.append((c, n))
        c += n
    if ffn_groups is None:
        ng = 3 if FC >= 12 else 2
        per = (len(subs) + ng - 1) // ng
        ffn_groups = [subs[i * per:(i + 1) * per] for i in range(ng)]
        ffn_groups = [g for g in ffn_groups if g]
    cfg["ffn_groups"] = ffn_groups
    cfg["GMAX"] = max(sum(n for _, n in g) for g in ffn_groups)
    off = {}
    p = 0

    def add(name, n):
        nonlocal p
        off[name] = (p, n)
        p += n
    add("mixg", 2 * KC)
    add("ffng", 2 * KC)
    add("pw1b", 2 * KC)
    add("dww", KC * CW)
    add("dwb", KC)
    add("lng", KC)
    add("lnb", KC)
    add("fdw", 2 * FC * FW)
    add("fdb", 2 * FC)
    add("qg", 1)
    add("kg", 1)
    cfg["voff"] = off
    cfg["NV"] = p
    return cfg


FULL = make_cfg()


def t5_bucket(dist):
    n = np.maximum(dist, 0)
    is_small = n < 16
    nf = np.maximum(n, 1).astype(np.float32)
    large = 16 + (np.log(nf / np.float32(16)) / np.float32(math.log(128 / 16)) * np.float32(16)).astype(np.int32)
    large = np.minimum(large, 31)
    return np.where(is_small, n, large)


def build_nc(cfg):
    D, F, NBC, TILE, NSLOT = cfg["D"], cfg["F"], cfg["NBC"], cfg["TILE"], cfg["NSLOT"]
    KC, FC, NH, NKV, QKV = cfg["KC"], cfg["FC"], cfg["NH"], cfg["NKV"], cfg["QKV"]
    GMAX = cfg["GMAX"]
    voff, NV = cfg["voff"], cfg["NV"]
    TT = TILE * 128
    HW = CW - 1
    UCW = HW + TT

    nc = bass.Bass("TRN2", target_bir_lowering=False)
    dt_in = lambda name, shape, dt=F32: nc.dram_tensor(name, shape, dt, kind="ExternalInput").ap()
    x_d = dt_in("x", [NBC * 128, D])
    pw1_d = dt_in("pw1_w", [D, 2 * D])
    pw2_d = dt_in("pw2_w", [D, D])
    wqkv_d = dt_in("wqkv", [D, QKV])
    wo_d = dt_in("wo", [D, D])
    win_d = [dt_in("win%d" % l, [D, 2 * F]) for l in range(2)]
    wout_d = [dt_in("wout%d" % l, [F, D]) for l in range(2)]
    vecs_d = dt_in("vecs", [128, NV])
    pw2b_d = dt_in("pw2b", [1, D])
    sinks_d = dt_in("sinks", [1, NH])
    biasT_d = dt_in("biasT", [128, NKV, 2, 512])
    ident_d = dt_in("ident", [128, 128], BF16)
    y_d = nc.dram_tensor("y", [NBC * 128, D], F32, kind="ExternalOutput").ap()

    with contextlib.ExitStack() as st:
        sb = lambda name, shape, dt: st.enter_context(nc.sbuf_tensor(name, shape, dt))
        UBW = max(KC * UCW, GMAX * TT, NH * TT, TILE * D)
        XR = sb("XR", [128, TILE, D], F32)
        HT = sb("HT", [128, KC, TT], BF16)
        UB = sb("UB", [128, UBW], BF16)
        WS = [sb("ws%d" % i, [128, 4096], BF16) for i in range(NSLOT)]
        VEC = sb("VEC", [128, NV], F32)
        IDT = sb("IDT", [128, 128], BF16)
        ONES = sb("ONES", [128, 128], BF16)
        SINKE = sb("SINKE", [128, NH], F32)
        CCu = sb("CCu", [128, KC, HW], BF16)
        GC = sb("GC", [128, 2 * FC, 2], F32)
        KCr = sb("KCr", [128, NKV, 128], BF16)
        VCr = sb("VCr", [128, NKV * 128], BF16)
        SS = sb("SS", [128, 8], F32)
        F32A = [sb("f32a%d" % i, [128, TT + 2], F32) for i in range(2)]
        F32Bt = sb("f32b", [128, 2, TT], F32)
        F32B = [F32Bt[:, 0, :], F32Bt[:, 1, :]]
        F32C = [sb("f32c%d" % i, [128, TT], F32) for i in range(2)]
        BF2 = [sb("bf2%d" % i, [128, 2, TT], BF16) for i in range(2)]
        QT = sb("QT", [128, 4, TT], BF16)
        NDG = 16
        DG = sb("DG", [128, NDG, 128], BF16)
        dgstate = {"i": 0}
        KTf = sb("KTf", [128, 2 * (128 + TT) // 1], F32)
        VTf = sb("VTf", [128, (TILE + 1) * 256], F32)
        KT = KTf[:].bitcast(BF16).rearrange("p (h t) -> p h t", t=128 + TT)
        VT = VTf[:].bitcast(BF16).rearrange("p (b n) -> p b n", n=512)
        MEAN = VTf[:, 0:TT]
        RSTD = VTf[:, TT:2 * TT]
        BBC = [KTf[:, 0:512], KTf[:, 512:1024]]
        ABI = F32Bt
        PS = [st.enter_context(nc.psum_tensor("ps%d" % i, [128, 512], F32)) for i in range(8)]

        P = Prog()
        B_XR = [Buf("xr%d" % b) for b in range(TILE)]
        B_HT = [Buf("ht%d" % c) for c in range(KC)]
        NUB = (UBW + 511) // 512
        B_UB = [Buf("ub%d" % i) for i in range(NUB)]
        B_WS = [Buf("ws%d" % i) for i in range(NSLOT)]
        B_PS = [Buf("ps%d" % i, excl=True) for i in range(8)]
        B_VEC, B_IDT, B_ONES, B_SINKE = Buf("vec"), Buf("idt"), Buf("ones"), Buf("sinke")
        B_CCu, B_GC, B_KCr, B_VCr = Buf("ccu"), Buf("gc"), Buf("kcr"), Buf("vcr")
        B_SS = Buf("ss")
        B_SSb = [Buf("ss%d" % i) for i in range(TILE)]
        B_A = [Buf("a0"), Buf("a1")]
        B_B = [Buf("b0"), Buf("b1")]
        B_C = [Buf("c0"), Buf("c1")]
        B_BF2 = [Buf("bf20"), Buf("bf21")]
        B_QT, B_KT, B_VT = Buf("qt"), Buf("kt"), Buf("vt")
        B_DG = [Buf("dg%d" % i) for i in range(NDG)]
        B_MEAN = B_RSTD = B_VT
        B_BBC = [B_KT, B_KT]

        def ub_bufs(lo, hi):
            return B_UB[lo // 512:(hi + 511) // 512]

        UBc = UB[:, 0:KC * UCW].rearrange("p (c t) -> p c t", t=UCW)
        UBa = UB[:, 0:GMAX * TT].rearrange("p (c t) -> p c t", t=TT)
        UBo = UB[:, 0:NH * TT].rearrange("p (h t) -> p h t", t=TT)
        UBx = UB[:, 0:TILE * D].rearrange("p (b d) -> p b d", d=D)

        def vcol(name, i):
            o, n = voff[name]
            return VEC[:, o + i:o + i + 1]

        rot = {"a": 0, "b": 0, "c": 0, "bf2": 0, "bbc": 0}

        def nxt(k):
            rot[k] ^= 1
            return rot[k]

        wstate = {"i": 0}

        def wload(src_ap, nk, ncols):
            s = wstate["i"] % NSLOT
            wstate["i"] += 1
            dst = WS[s][:, 0:nk * ncols].rearrange("p (k n) -> p k n", n=ncols)
            P.dma("pool", lambda e: e.dma_start(out=dst, in_=src_ap), writes=[B_WS[s]])
            return dst, B_WS[s]

        def wtile(w_ap, r0, nk, c0, ncols):
            v = w_ap[r0 * 128:(r0 + nk) * 128, c0:c0 + ncols].rearrange("(k p) n -> p k n", p=128)
            return wload(v, nk, ncols)

        P.dma("sp", lambda e: e.dma_start(out=VEC[:], in_=vecs_d), writes=[B_VEC])
        P.dma("sp", lambda e: e.dma_start(out=IDT[:], in_=ident_d), writes=[B_IDT])
        dbg = cfg.get("dbg", 0)
        P.op("dve", lambda e: e.memset(ONES[:], 1.0), writes=[B_ONES])
        if not dbg & 1:
            P.dma("sp", lambda e: e.dma_start(out=SINKE[:], in_=sinks_d.partition_broadcast(128)), writes=[B_SINKE])
            P.op("act", lambda e: e.activation(out=SINKE[:], in_=SINKE[:], func=AF.Exp), reads=[B_SINKE], writes=[B_SINKE])
        if not dbg & 2:
            P.op("dve", lambda e: e.memset(CCu[:], 0.0), writes=[B_CCu])
            P.op("dve", lambda e: e.memset(GC[:], 0.0), writes=[B_GC])
            P.op("dve", lambda e: e.memset(KCr[:], 0.0), writes=[B_KCr])
            P.op("dve", lambda e: e.memset(VCr[:], 0.0), writes=[B_VCr])

        B_ABI = [B_B[0], B_B[1]]

        def ktiles(nchunks, ncols):
            nkmax = max(1, 4096 // ncols)
            out = []
            k = 0
            while k < nchunks:
                n = min(nkmax, nchunks - k)
                out.append((k, n))
                k += n
            return out

        def fm_pieces(w_ap, c0, nch, banks, T):
            pieces = []
            for (k0, nk) in ktiles(KC, nch * 128):
                def piece(k0=k0, nk=nk):
                    wt, wb = wtile(w_ap, k0, nk, c0, nch * 128)
                    for i in range(nch):
                        for k in range(nk):
                            kk = k0 + k
                            P.op("pe", lambda e, o=PS[banks[i]][:, 0:T], l=wt[:, k, i * 128:(i + 1) * 128],
                                 r=HT[:, kk, 0:T], st_=(kk == 0), sp_=(kk == KC - 1): e.matmul(o, l, r, start=st_, stop=sp_),
                                 reads=[wb, B_HT[kk]], writes=[B_PS[banks[i]]])
                pieces.append(piece)
            return pieces

        def fm_matmul(w_ap, c0, nch, banks, T):
            for p in fm_pieces(w_ap, c0, nch, banks, T):
                p()

        def tm_matmul(w_ap, r0, kchunks, lhs_fn, lhs_bufs_fn, nb, bankset, c0, ncols):
            for (k0, nk) in ktiles(kchunks, ncols):
                wt, wb = wtile(w_ap, r0 + k0, nk, c0, ncols)
                for b in range(nb):
                    for k in range(nk):
                        kk = k0 + k
                        P.op("pe", lambda e, o=PS[bankset[b]][:, 0:ncols], l=lhs_fn(kk, b), r=wt[:, k, :],
                             st_=(kk == 0), sp_=(kk == kchunks - 1): e.matmul(o, l, r, start=st_, stop=sp_),
                             reads=[wb] + lhs_bufs_fn(kk), writes=[B_PS[bankset[b]]])

        def rmsnorm_to_HT(nb, gname, layer):
            for b in range(nb):
                junk = UBx[:, b, :]
                jb = ub_bufs(b * D, (b + 1) * D)
                P.op("act", lambda e, o=junk, i=XR[:, b, :], a=SS[:, b:b + 1]: e.activation(
                    out=o, in_=i, func=AF.Square, accum_out=a), reads=[B_XR[b]], writes=jb + [B_SSb[b]])
                P.op("act", lambda e, o=SS[:, 4 + b:5 + b], i=SS[:, b:b + 1]: e.activation(
                    out=o, in_=i, func=AF.Sqrt, scale=1.0 / D, bias=EPS), reads=[B_SSb[b]], writes=[B_SSb[b]])
                P.op("dve", lambda e, o=SS[:, 4 + b:5 + b]: e.reciprocal(o, o), reads=[B_SSb[b]], writes=[B_SSb[b]])
                P.op("dve", lambda e, o=junk, i=XR[:, b, :], r=SS[:, 4 + b:5 + b]: e.tensor_scalar(o, i, r, None, ALU.mult),
                     reads=[B_XR[b], B_SSb[b]], writes=jb)
            for b in range(nb):
                junk = UBx[:, b, :]
                jb = ub_bufs(b * D, (b + 1) * D)
                for c0 in range(0, KC, 8):
                    ncb = min(8, KC - c0)
                    bank = 4 + ((c0 // 8) % 4)
                    pv = PS[bank][:].bitcast(BF16)
                    for c in range(ncb):
                        P.op("pe", lambda e, o=pv[:, c * 128:(c + 1) * 128], i=junk[:, (c0 + c) * 128:(c0 + c + 1) * 128]:
                             e.transpose(o, i, IDT[:]), reads=jb + [B_IDT], writes=[B_PS[bank]])
                    for c in range(ncb):
                        gcol = vcol(gname, layer * KC + c0 + c)
                        o = HT[:, c0 + c, b * 128:(b + 1) * 128]
                        i = pv[:, c * 128:(c + 1) * 128]
                        if ((c0 // 8) + b) % 2 == 0:
                            P.op("act", lambda e, o=o, i=i, g=gcol: e.activation(out=o, in_=i, func=AF.Copy, scale=g),
                                 reads=[B_PS[bank], B_VEC], writes=[B_HT[c0 + c]])
                        else:
                            P.op("dve", lambda e, o=o, i=i, g=gcol: e.tensor_scalar(o, i, g, None, ALU.mult),
                                 reads=[B_PS[bank], B_VEC], writes=[B_HT[c0 + c]])

        def resid_evac(nb, bankset, c0, ncols, bias_ap=None, bias_buf=None):
            for b in range(nb):
                dst = XR[:, b, c0:c0 + ncols]
                P.op("dve", lambda e, d=dst, p=PS[bankset[b]][:, 0:ncols]: e.tensor_tensor(d, p, d, ALU.add),
                     reads=[B_PS[bankset[b]], B_XR[b]], writes=[B_XR[b]])
                if bias_ap is not None:
                    P.op("dve", lambda e, d=dst, bi=bias_ap: e.tensor_tensor(d, d, bi, ALU.add),
                         reads=[B_XR[b], bias_buf], writes=[B_XR[b]])

        def tm_project(w_ap, r0, kchunks, lhs_fn, lhs_bufs_fn, nb, bias=False):
            ncolchunks = D // 512
            for j in range(ncolchunks):
                bankset = [0, 1, 2, 3] if j % 2 == 0 else [4, 5, 6, 7]
                bap = bbuf = None
                if bias:
                    q = nxt("bbc")
                    P.dma("sp", lambda e, o=BBC[q], i=pw2b_d[:, j * 512:(j + 1) * 512].partition_broadcast(128):
                          e.dma_start(out=o, in_=i), writes=[B_BBC[q]])
                    bap, bbuf = BBC[q], B_BBC[q]
                tm_matmul(w_ap, r0, kchunks, lhs_fn, lhs_bufs_fn, nb, bankset, j * 512, 512)
                resid_evac(nb, bankset, j * 512, 512, bap, bbuf)

        def conformer(nb):
            T = nb * 128
            sub = cfg.get("sub", 9)
            if sub >= 1:
                rmsnorm_to_HT(nb, "mixg", 0)
            if sub < 2:
                return
            P.op("act", lambda e: e.activation(out=UBc[:, :, 0:HW], in_=CCu[:, :, :], func=AF.Copy),
                 reads=[B_CCu], writes=ub_bufs(0, KC * UCW))
            ab, gb, cvb = [0, 1], [2, 3], [4, 5]
            groups = [(c0, min(2, KC - c0)) for c0 in range(0, KC, 2)]

            def glu_group(c0, nch):
                fm_matmul(pw1_d, c0 * 128, nch, ab, T)
                fm_matmul(pw1_d, D + c0 * 128, nch, gb, T)
                for i in range(nch):
                    c = c0 + i
                    a = nxt("a")
                    ubb = ub_bufs(c * UCW, (c + 1) * UCW)
                    P.op("act", lambda e, o=F32A[a][:, 0:T], i_=PS[gb[i]][:, 0:T], bi=vcol("pw1b", KC + c): e.activation(
                        out=o, in_=i_, func=AF.Sigmoid, bias=bi), reads=[B_PS[gb[i]], B_VEC], writes=[B_A[a]])
                    P.op("dve", lambda e, o=UBc[:, c, HW:HW + T], p=PS[ab[i]][:, 0:T], bi=vcol("pw1b", c), sg=F32A[a][:, 0:T]:
                         e.scalar_tensor_tensor(o, p, bi, sg, ALU.add, ALU.mult),
                         reads=[B_PS[ab[i]], B_A[a], B_VEC], writes=ubb)

            def conv_group(c0, nch):
                for i in range(nch):
                    c = c0 + i
                    ubb = ub_bufs(c * UCW, (c + 1) * UCW)
                    cb = cvb[i]
                    for k in range(CW):
                        sl = dgstate["i"] % NDG
                        dgstate["i"] += 1
                        P.op("dve", lambda e, o=DG[:, sl, :], w=vcol("dww", c * CW + k): e.tensor_scalar(o, IDT[:], w, None, ALU.mult),
                             reads=[B_IDT, B_VEC], writes=[B_DG[sl]])
                        P.op("pe", lambda e, o=PS[cb][:, 0:T], l=DG[:, sl, :], r=UBc[:, c, k:k + T], st_=(k == 0), sp_=(k == CW - 1):
                             e.matmul(o, l, r, start=st_, stop=sp_), reads=[B_DG[sl]] + ubb, writes=[B_PS[cb]])
                    P.op("act", lambda e, o=CCu[:, c, :], i_=UBc[:, c, T:T + HW]: e.activation(out=o, in_=i_, func=AF.Copy),
                         reads=ubb, writes=[B_CCu])
                    P.op("act", lambda e, o=UBc[:, c, HW:HW + T], i_=PS[cb][:, 0:T], bi=vcol("dwb", c): e.activation(
                        out=o, in_=i_, func=AF.Identity, bias=bi), reads=[B_PS[cb], B_VEC], writes=ubb)
                    q = nxt("bf2")
                    P.op("act", lambda e, o=BF2[q][:, 0, 0:T], i_=PS[cb][:, 0:T], bi=vcol("dwb", c): e.activation(
                        out=o, in_=i_, func=AF.Square, bias=bi), reads=[B_PS[cb], B_VEC], writes=[B_BF2[q]])
                    P.op("pe", lambda e, o=PS[6][:, 0:T], r=UBc[:, c, HW:HW + T], st_=(c == 0), sp_=(c == KC - 1):
                         e.matmul(o, ONES[:], r, start=st_, stop=sp_), reads=ubb + [B_ONES], writes=[B_PS[6]])
                    P.op("pe", lambda e, o=PS[7][:, 0:T], r=BF2[q][:, 0, 0:T], st_=(c == 0), sp_=(c == KC - 1):
                         e.matmul(o, ONES[:], r, start=st_, stop=sp_), reads=[B_BF2[q], B_ONES], writes=[B_PS[7]])

            for j in range(len(groups) + 1):
                if j < len(groups):
                    glu_group(*groups[j])
                if j >= 1:
                    conv_group(*groups[j - 1])
            if sub < 4:
                return
            c_ = nxt("c")
            msq = F32C[c_][:, 0:T]
            mean = MEAN[:, 0:T]
            rstd = RSTD[:, 0:T]
            P.op("dve", lambda e: e.tensor_scalar(mean, PS[6][:, 0:T], 1.0 / D, None, ALU.mult),
                 reads=[B_PS[6]], writes=[B_MEAN])
            P.op("dve", lambda e: e.tensor_tensor(msq, mean, mean, ALU.mult), reads=[B_MEAN], writes=[B_C[c_]])
            P.op("dve", lambda e: e.scalar_tensor_tensor(rstd, PS[7][:, 0:T], 1.0 / D, msq, ALU.mult, ALU.subtract),
                 reads=[B_PS[7], B_C[c_]], writes=[B_RSTD])
            P.op("act", lambda e: e.activation(out=rstd, in_=rstd, func=AF.Sqrt, bias=EPS), reads=[B_RSTD], writes=[B_RSTD])
            P.op("dve", lambda e: e.reciprocal(rstd, rstd), reads=[B_RSTD], writes=[B_RSTD])
            for c in range(KC):
                ubb = ub_bufs(c * UCW, (c + 1) * UCW)
                a = nxt("a")
                t = F32A[a][:, 0:T]
                P.op("dve", lambda e, t=t, v=UBc[:, c, HW:HW + T]: e.tensor_tensor(t, v, mean, ALU.subtract),
                     reads=ubb + [B_MEAN], writes=[B_A[a]])
                P.op("dve", lambda e, t=t: e.tensor_tensor(t, t, rstd, ALU.mult), reads=[B_A[a], B_RSTD], writes=[B_A[a]])
                P.op("act", lambda e, t=t, o=HT[:, c, 0:T], g=vcol("lng", c), bi=vcol("lnb", c): e.activation(
                    out=o, in_=t, func=AF.Silu, scale=g, bias=bi), reads=[B_A[a], B_VEC], writes=[B_HT[c]])
            if sub < 5:
                return
            tm_project(pw2_d, 0, KC, lambda kk, b: HT[:, kk, b * 128:(b + 1) * 128], lambda kk: [B_HT[kk]], nb, bias=True)

        def ffn(nb, l):
            T = nb * 128
            rmsnorm_to_HT(nb, "ffng", l)
            for grp in cfg["ffn_groups"]:
                gch0 = grp[0][0]
                gn = sum(n for _, n in grp)
                for (c0, nch) in grp:
                    fm_matmul(win_d[l], c0 * 128, nch, [0, 1, 2, 3], T)
                    fm_matmul(win_d[l], F + c0 * 128, nch, [4, 5, 6, 7], T)
                    for i in range(nch):
                        c = c0 + i
                        cg = c - gch0
                        a = nxt("a")
                        G = F32A[a]
                        gcr = GC[:, l * FC + c, :]
                        P.op("act", lambda e, o=G[:, 0:2], i_=gcr: e.activation(out=o, in_=i_, func=AF.Copy),
                             reads=[B_GC], writes=[B_A[a]])
                        P.op("act", lambda e, o=G[:, 2:2 + T], i_=PS[i][:, 0:T]: e.activation(out=o, in_=i_, func=AF.Copy),
                             reads=[B_PS[i]], writes=[B_A[a]])
                        P.op("act", lambda e, o=gcr, i_=G[:, T:T + 2]: e.activation(out=o, in_=i_, func=AF.Copy),
                             reads=[B_A[a]], writes=[B_GC])
                        bb = nxt("b")
                        acc = F32B[bb][:, 0:T]
                        wi = (l * FC + c) * FW
                        P.op("dve", lambda e, o=acc, i_=G[:, 2:2 + T], w=vcol("fdw", wi + 2), bi=vcol("fdb", l * FC + c):
                             e.tensor_scalar(o, i_, w, bi, ALU.mult, ALU.add), reads=[B_A[a], B_VEC], writes=[B_B[bb]])
                        P.op("dve", lambda e, o=acc, i_=G[:, 1:1 + T], w=vcol("fdw", wi + 1):
                             e.scalar_tensor_tensor(o, i_, w, o, ALU.mult, ALU.add),
                             reads=[B_A[a], B_VEC, B_B[bb]], writes=[B_B[bb]])
                        P.op("dve", lambda e, o=acc, i_=G[:, 0:T], w=vcol("fdw", wi + 0):
                             e.scalar_tensor_tensor(o, i_, w, o, ALU.mult, ALU.add),
                             reads=[B_A[a], B_VEC, B_B[bb]], writes=[B_B[bb]])
                        cc = nxt("c")
                        S = F32C[cc][:, 0:T]
                        P.op("act", lambda e, o=S, i_=acc: e.activation(out=o, in_=i_, func=AF.Silu),
                             reads=[B_B[bb]], writes=[B_C[cc]])
                        P.op("dve", lambda e, o=UBa[:, cg, 0:T], s_=S, v=PS[4 + i][:, 0:T]: e.tensor_tensor(o, s_, v, ALU.mult),
                             reads=[B_C[cc], B_PS[4 + i]], writes=ub_bufs(cg * TT, (cg + 1) * TT))
                tm_project(wout_d[l], gch0, gn, lambda kk, b: UBa[:, kk, b * 128:(b + 1) * 128],
                           lambda kk: ub_bufs(kk * TT, (kk + 1) * TT), nb)

        def qk_norm(bank, T, gname, dst_ap, dst_bufs, ssbank):
            q = nxt("bf2")
            sq = BF2[q][:, 0, 0:T]
            P.op("act", lambda e: e.activation(out=sq, in_=PS[bank][:, 0:T], func=AF.Square),
                 reads=[B_PS[bank]], writes=[B_BF2[q]])
            P.op("pe", lambda e: e.matmul(PS[ssbank][:, 0:T], ONES[:], sq, start=True, stop=True),
                 reads=[B_BF2[q], B_ONES], writes=[B_PS[ssbank]])
            cc = nxt("c")
            r = F32C[cc][:, 0:T]
            P.op("act", lambda e: e.activation(out=r, in_=PS[ssbank][:, 0:T], func=AF.Ln, scale=1.0 / 128, bias=EPS),
                 reads=[B_PS[ssbank]], writes=[B_C[cc]])
            P.op("act", lambda e: e.activation(out=r, in_=r, func=AF.Exp, scale=-0.5), reads=[B_C[cc]], writes=[B_C[cc]])
            P.op("dve", lambda e: e.scalar_tensor_tensor(dst_ap, PS[bank][:, 0:T], vcol(gname, 0), r, ALU.mult, ALU.mult),
                 reads=[B_PS[bank], B_C[cc], B_VEC], writes=dst_bufs)

        def attention(nb, first_tile):
            T = nb * 128
            scale = 1.0 / math.sqrt(128.0)
            rmsnorm_to_HT(nb, "mixg", 1)
            QTs = [QT, DG[:].rearrange("p a b -> p (a b)").rearrange("p (h t) -> p h t", t=TT)]
            B_QTs = [[B_QT], list(B_DG)]
            rounds = [(r0, min(4, NKV - r0)) for r0 in range(0, NKV, 4)]

            def k_pieces(r0, nkv):
                return fm_pieces(wqkv_d, NH * 128 + r0 * 128, nkv, [0, 1, 2, 3], T)

            def k_finish(r0, nkv):
                P.op("act", lambda e, o=KT[:, 0:nkv, 0:128], i_=KCr[:, r0:r0 + nkv, :]: e.activation(out=o, in_=i_, func=AF.Copy),
                     reads=[B_KCr], writes=[B_KT])
                for i in range(nkv):
                    qk_norm(i, T, "kg", KT[:, i, 128:128 + T], [B_KT], 7)
                P.op("act", lambda e, o=KCr[:, r0:r0 + nkv, :], i_=KT[:, 0:nkv, T:T + 128]: e.activation(out=o, in_=i_, func=AF.Copy),
                     reads=[B_KT], writes=[B_KCr])

            def v_all(r0, nkv):
                P.op("act", lambda e, o=VT[:, 0, 0:nkv * 128], i_=VCr[:, r0 * 128:(r0 + nkv) * 128]: e.activation(out=o, in_=i_, func=AF.Copy),
                     reads=[B_VCr], writes=[B_VT])
                tm_matmul(wqkv_d, 0, KC, lambda kk, b: HT[:, kk, b * 128:(b + 1) * 128], lambda kk: [B_HT[kk]],
                          nb, [0, 1, 2, 3], (NH + NKV) * 128 + r0 * 128, nkv * 128)
                for b in range(nb):
                    P.op("act", lambda e, o=VT[:, 1 + b, 0:nkv * 128], i_=PS[b][:, 0:nkv * 128]: e.activation(out=o, in_=i_, func=AF.Copy),
                         reads=[B_PS[b]], writes=[B_VT])
                P.op("act", lambda e, o=VCr[:, r0 * 128:(r0 + nkv) * 128], i_=VT[:, nb, 0:nkv * 128]: e.activation(out=o, in_=i_, func=AF.Copy),
                     reads=[B_VT], writes=[B_VCr])

            def q_pieces(kv):
                return fm_pieces(wqkv_d, kv * 512, 4, [0, 1, 2, 3], T)

            def q_finish(kv):
                for i in range(4):
                    qk_norm(i, T, "qg", QTs[kv % 2][:, i, 0:T], B_QTs[kv % 2], 7)

            def attn_blocks(kv, jj, fillers):
                QTc, B_QTc = QTs[kv % 2], B_QTs[kv % 2]
                nf = len(fillers)
                per = [(nf + nb - 1 - b) // nb for b in range(nb)]
                for b in range(nb):
                    has_prev = not (first_tile and b == 0)
                    rhs = QTc[:, :, b * 128:(b + 1) * 128]
                    q = nxt("bf2")
                    parts = ([0] if has_prev else []) + [1]
                    for w in parts:
                        ko = b * 128 if w == 0 else 128 + b * 128
                        sbank = 4 + w
                        P.op("pe", lambda e, o=PS[sbank][:, :], l=KT[:, jj, ko:ko + 128], r=rhs: e.matmul(o, l, r, start=True, stop=True),
                             reads=[B_KT] + B_QTc, writes=[B_PS[sbank]])
                        a = nxt("a")
                        t = F32A[a][:, 0:512]
                        P.op("dve", lambda e, t=t, p=PS[sbank][:, :], bi=ABI[:, w, :]: e.scalar_tensor_tensor(
                            t, p, scale, bi, ALU.mult, ALU.add), reads=[B_PS[sbank]] + B_ABI, writes=[B_A[a]])
                        P.op("act", lambda e, t=t, o=BF2[q][:, w, :]: e.activation(out=o, in_=t, func=AF.Exp),
                             reads=[B_A[a]], writes=[B_BF2[q]])
                    for _ in range(per[b]):
                        fillers.pop(0)()
                    for wi_, w in enumerate(parts):
                        P.op("pe", lambda e, r=BF2[q][:, w, :], st_=(wi_ == 0), sp_=(wi_ == len(parts) - 1):
                             e.matmul(PS[6][:, :], ONES[:], r, start=st_, stop=sp_),
                             reads=[B_BF2[q], B_ONES], writes=[B_PS[6]])
                    for wi_, w in enumerate(parts):
                        vb = b if w == 0 else b + 1
                        P.op("pe", lambda e, l=VT[:, vb, jj * 128:(jj + 1) * 128], r=BF2[q][:, w, :],
                             st_=(wi_ == 0), sp_=(wi_ == len(parts) - 1): e.matmul(PS[7][:, :], l, r, start=st_, stop=sp_),
                             reads=[B_BF2[q], B_VT], writes=[B_PS[7]])
                    cc = nxt("c")
                    rc = F32C[cc][:, 0:512]
                    for i in range(4):
                        P.op("act", lambda e, o=rc[:, i * 128:(i + 1) * 128], p=PS[6][:, i * 128:(i + 1) * 128],
                             sk=SINKE[:, kv * 4 + i:kv * 4 + i + 1]: e.activation(out=o, in_=p, func=AF.Ln, bias=sk),
                             reads=[B_PS[6], B_SINKE], writes=[B_C[cc]])
                    P.op("act", lambda e, rc=rc: e.activation(out=rc, in_=rc, func=AF.Exp, scale=-1.0), reads=[B_C[cc]], writes=[B_C[cc]])
                    dst = UBo[:, kv * 4:kv * 4 + 4, b * 128:(b + 1) * 128]
                    P.op("dve", lambda e, rc=rc, dst=dst: e.tensor_tensor(
                        dst, PS[7][:, :].rearrange("p (h t) -> p h t", t=128),
                        rc.rearrange("p (h t) -> p h t", t=128), ALU.mult),
                        reads=[B_PS[7], B_C[cc]], writes=ub_bufs(kv * 4 * TT, (kv * 4 + 4) * TT))
                while fillers:
                    fillers.pop(0)()

            for p in k_pieces(*rounds[0]):
                p()
            k_finish(*rounds[0])
            v_all(*rounds[0])
            for p in q_pieces(0):
                p()
            q_finish(0)
            for kv in range(NKV):
                ri, jj = divmod(kv, 4)
                P.dma("sp", lambda e, i_=biasT_d[:, kv, :, :]: e.dma_start(out=ABI[:], in_=i_), writes=B_ABI)
                fillers = []
                nxt_round = (kv + 1 < NKV) and ((kv + 1) % 4 == 0)
                if kv + 1 < NKV:
                    fillers = k_pieces(*rounds[ri + 1]) if nxt_round else q_pieces(kv + 1)
                attn_blocks(kv, jj, fillers)
                if kv + 1 < NKV:
                    if nxt_round:
                        k_finish(*rounds[ri + 1])
                        v_all(*rounds[ri + 1])
                        for p in q_pieces(kv + 1):
                            p()
                    q_finish(kv + 1)
            tm_project(wo_d, 0, NH, lambda kk, b: UBo[:, kk, b * 128:(b + 1) * 128],
                       lambda kk: ub_bufs(kk * TT, (kk + 1) * TT), nb)

        stores = []
        blk = 0
        first = True
        stages = cfg.get("stages", 4)
        ntiles = (NBC + TILE - 1) // TILE
        sched = [NBC // ntiles + (1 if i < NBC % ntiles else 0) for i in range(ntiles)]
        for nb in sched:
            for b in range(nb):
                P.dma("sp", lambda e, o=XR[:, b, :], i_=x_d[(blk + b) * 128:(blk + b + 1) * 128, :]: e.dma_start(out=o, in_=i_),
                      writes=[B_XR[b]])
            if stages >= 1:
                conformer(nb)
            if stages >= 2:
                ffn(nb, 0)
            if stages >= 3:
                attention(nb, first)
            if stages >= 4:
                ffn(nb, 1)
            for b in range(nb):
                stores.append(P.dma("sp", lambda e, o=y_d[(blk + b) * 128:(blk + b + 1) * 128, :], i_=XR[:, b, :]:
                                    e.dma_start(out=o, in_=i_), reads=[B_XR[b]]))
            blk += nb
            first = False
        P.barrier_wait("sp", stores)
        P.emit(nc)
    return nc


def fm(v):
    v = np.asarray(v, dtype=np.float32)
    lead = v.shape[:-1]
    C = v.shape[-1] // 128
    a = v.reshape(lead + (C, 128))
    a = np.moveaxis(a, -1, 0)
    return a


def prep_shared(cfg, inp):
    D, F, KC, FC, NH, NKV = cfg["D"], cfg["F"], cfg["KC"], cfg["FC"], cfg["NH"], cfg["NKV"]
    NV, voff = cfg["NV"], cfg["voff"]
    vecs = np.zeros((128, NV), np.float32)

    def put(name, arr):
        o, n = voff[name]
        vecs[:, o:o + n] = arr.reshape(128, n)
    put("mixg", fm(inp["mix_norm_g"]))
    put("ffng", fm(inp["ffn_norm_g"]))
    put("pw1b", fm(inp["conv_pw1_b"][0]))
    put("dww", np.transpose(fm(inp["conv_dw_w"][0]), (0, 2, 1)))
    put("dwb", fm(inp["conv_dw_b"][0]))
    put("lng", fm(inp["conv_ln_g"][0]))
    put("lnb", fm(inp["conv_ln_b"][0]))
    put("fdw", np.transpose(fm(inp["ffn_dw_w"]), (0, 1, 3, 2)))
    put("fdb", fm(inp["ffn_dw_b"]))
    put("qg", np.asarray(inp["attn_q_norm_g"][0], np.float32).reshape(128, 1))
    put("kg", np.asarray(inp["attn_k_norm_g"][0], np.float32).reshape(128, 1))
    rb = np.asarray(inp["rel_bias"], np.float32)
    tbl = np.concatenate([rb, np.full((1, NH), NEG, np.float32)], axis=0)
    qi = np.arange(128)[None, :]
    kj = np.arange(128)[:, None]
    idx = np.zeros((2, 128, 128), np.int64)
    d_prev = qi - kj + 128
    d_cur = qi - kj
    idx[0] = np.where((d_prev >= 0) & (d_prev < 128), t5_bucket(d_prev), 32)
    idx[1] = np.where((d_cur >= 0) & (d_cur < 128), t5_bucket(d_cur), 32)
    g = tbl[idx]
    g = g.reshape(2, 128, 128, NKV, 4)
    biasT = np.ascontiguousarray(np.transpose(g, (1, 3, 0, 4, 2))).reshape(128, NKV, 2, 512)
    shared = {
        "pw1_w": np.ascontiguousarray(inp["conv_pw1_w"][0], dtype=np.float32),
        "pw2_w": np.ascontiguousarray(inp["conv_pw2_w"][0], dtype=np.float32),
        "wqkv": np.ascontiguousarray(inp["attn_w_qkv"][0], dtype=np.float32),
        "wo": np.ascontiguousarray(inp["attn_w_o"][0], dtype=np.float32),
        "win0": np.ascontiguousarray(inp["ffn_w_in"][0], dtype=np.float32),
        "win1": np.ascontiguousarray(inp["ffn_w_in"][1], dtype=np.float32),
        "wout0": np.ascontiguousarray(inp["ffn_w_out"][0], dtype=np.float32),
        "wout1": np.ascontiguousarray(inp["ffn_w_out"][1], dtype=np.float32),
        "vecs": vecs,
        "pw2b": np.ascontiguousarray(inp["conv_pw2_b"][0], dtype=np.float32).reshape(1, D),
        "sinks": np.ascontiguousarray(inp["attn_sinks"][0], dtype=np.float32).reshape(1, NH),
        "biasT": biasT.astype(np.float32),
        "ident": np.eye(128, dtype=np.float32).astype(ml_dtypes.bfloat16),
    }
    return shared


def core_starts(nblk_total, nbc, ncores):
    step = (nblk_total) // ncores
    return [min(step * c, nblk_total - nbc) for c in range(ncores)]


_NC_CACHE = {}


def run_cfg(cfg, inp, ncores):
    D, NBC = cfg["D"], cfg["NBC"]
    x = np.asarray(inp["x"], dtype=np.float32)
    S = x.shape[1]
    nblk = S // 128
    starts = core_starts(nblk, NBC, ncores)
    shared = prep_shared(cfg, inp)
    in_maps = []
    for c in range(ncores):
        m = dict(shared)
        m["x"] = np.ascontiguousarray(x[0, starts[c] * 128:(starts[c] + NBC) * 128, :])
        in_maps.append(m)
    key = (D, cfg["F"], NBC, cfg["TILE"], cfg["NSLOT"])
    if key not in _NC_CACHE:
        _NC_CACHE[key] = build_nc(cfg)
    nc = _NC_CACHE[key]
    res = run_bass_kernel_spmd(nc, in_maps, core_ids=list(range(ncores)))
    out = np.empty((1, S, D), np.float32)
    for c in range(ncores):
        y = res.results[c]["y"]
        lo = 0 if starts[c] == 0 else 2
        out[0, (starts[c] + lo) * 128:(starts[c] + NBC) * 128, :] = y[lo * 128:, :]
    return out


def kernel(**inputs):
    return run_cfg(FULL, inputs, 8)
```

```python
import contextlib
import math
import numpy as np
import ml_dtypes
import concourse.bass as bass
import concourse.mybir as mybir
from concourse.bass_utils import run_bass_kernel_spmd

F32 = mybir.dt.float32
BF16 = mybir.dt.bfloat16
AF = mybir.ActivationFunctionType
ALU = mybir.AluOpType

ENGS = ("pe", "act", "dve", "pool", "sp")
EPS = 1e-6
CW = 31
FW = 3
NEG = -30000.0


class Buf:
    __slots__ = ("name", "w", "r", "excl")

    def __init__(self, name, excl=False):
        self.name = name
        self.w = None
        self.r = {}
        self.excl = excl


class Op:
    __slots__ = ("eng", "fn", "deps", "flag", "semval", "dsem", "is_dma")

    def __init__(self, eng, fn, is_dma):
        self.eng = eng
        self.fn = fn
        self.deps = []
        self.flag = False
        self.semval = 0
        self.dsem = None
        self.is_dma = is_dma


class Prog:
    def __init__(self, n_dma_sems=16):
        self.q = {e: [] for e in ENGS}
        self.n_dma_sems = n_dma_sems
        self.dma_last = {e: [None] * n_dma_sems for e in ENGS}
        self.dma_uses = {e: [0] * n_dma_sems for e in ENGS}
        self.dma_rr = {e: 0 for e in ENGS}

    def _collect(self, op, reads, writes):
        deps = {}
        for b in reads:
            if b.w is not None:
                deps[id(b.w)] = b.w
            if b.excl:
                for k_, r in b.r.items():
                    if r.eng != op.eng:
                        deps[id(r)] = r
        for b in writes:
            if b.w is not None:
                deps[id(b.w)] = b.w
            for r in b.r.values():
                deps[id(r)] = r
        for d in deps.values():
            if d is op:
                continue
            if d.eng == "pe" and op.eng == "pe" and not d.is_dma and not op.is_dma:
                continue
            op.deps.append(d)
            if not d.is_dma:
                d.flag = True
        key = op.eng + ("_d" if op.is_dma else "")
        for b in reads:
            b.r[key] = op
        for b in writes:
            b.w = op
            b.r = {}

    def op(self, eng, fn, reads=(), writes=()):
        o = Op(eng, fn, False)
        self._collect(o, reads, writes)
        self.q[eng].append(o)
        return o

    def dma(self, eng, fn, reads=(), writes=()):
        o = Op(eng, fn, True)
        s = self.dma_rr[eng]
        self.dma_rr[eng] = (s + 1) % self.n_dma_sems
        prev = self.dma_last[eng][s]
        self._collect(o, reads, writes)
        if prev is not None and prev not in o.deps:
            o.deps.append(prev)
        self.dma_uses[eng][s] += 1
        o.dsem = (eng, s)
        o.semval = 16 * self.dma_uses[eng][s]
        self.dma_last[eng][s] = o
        self.q[eng].append(o)
        return o

    def barrier_wait(self, eng, ops):
        o = Op(eng, None, False)
        for d in ops:
            o.deps.append(d)
            if not d.is_dma:
                d.flag = True
        self.q[eng].append(o)
        return o

    def emit(self, nc):
        for e in ENGS:
            c = 0
            for o in self.q[e]:
                if o.is_dma or o.fn is None:
                    continue
                if o.flag:
                    c += 1
                    o.semval = c
        dma_engs = [e for e in ENGS if any(o.is_dma for o in self.q[e])]
        with contextlib.ExitStack() as st:
            esem = {e: st.enter_context(nc.semaphore("s_" + e)) for e in ENGS}
            dsem = {(e, i): st.enter_context(nc.semaphore("d_%s_%d" % (e, i)))
                    for e in dma_engs for i in range(self.n_dma_sems)}
            block = st.enter_context(nc.Block())
            handles = {"pe": block.tensor, "act": block.scalar, "dve": block.vector,
                       "pool": block.gpsimd, "sp": block.sync}

            def run(e):
                ops = self.q[e]
                mysem = esem[e]

                def body(eng):
                    seen = {}
                    for o in ops:
                        for d in o.deps:
                            if d.is_dma:
                                key = ("d", d.dsem)
                                sem = dsem[d.dsem]
                            else:
                                key = ("e", d.eng)
                                sem = esem[d.eng]
                            if seen.get(key, 0) < d.semval:
                                eng.wait_ge(sem, d.semval)
                                seen[key] = d.semval
                        if o.fn is None:
                            continue
                        ins = o.fn(eng)
                        if o.is_dma:
                            ins.then_inc(dsem[o.dsem], 16)
                        elif o.flag:
                            ins.then_inc(mysem, 1)
                return body

            for e in ENGS:
                if self.q[e]:
                    handles[e](run(e))


def make_cfg(D=4096, F=11008, NBC=18, TILE=4, NSLOT=3, ffn_groups=None):
    KC = D // 128
    FC = F // 128
    NH = D // 128
    NKV = NH // 4
    cfg = dict(D=D, F=F, NBC=NBC, TILE=TILE, NSLOT=NSLOT, KC=KC, FC=FC, NH=NH, NKV=NKV,
               QKV=(NH + 2 * NKV) * 128)
    subs = []
    c = 0
    while c < FC:
        n = min(4, FC - c)
        subs.append((c, n))
        c += n
    if ffn_groups is None:
        ng = 3 if FC >= 12 else 2
        per = (len(subs) + ng - 1) // ng
        ffn_groups = [subs[i * per:(i + 1) * per] for i in range(ng)]
        ffn_groups = [g for g in ffn_groups if g]
    cfg["ffn_groups"] = ffn_groups
    cfg["GMAX"] = max(sum(n for _, n in g) for g in ffn_groups)
    off = {}
    p = 0

    def add(name, n):
        nonlocal p
        off[name] = (p, n)
        p += n
    add("mixg", 2 * KC)
    add("ffng", 2 * KC)
    add("pw1b", 2 * KC)
    add("dww", KC * CW)
    add("dwb", KC)
    add("lng", KC)
    add("lnb", KC)
    add("fdw", 2 * FC * FW)
    add("fdb", 2 * FC)
    add("qg", 1)
    add("kg", 1)
    cfg["voff"] = off
    cfg["NV"] = p
    return cfg


FULL = make_cfg()


def t5_bucket(dist):
    n = np.maximum(dist, 0)
    is_small = n < 16
    nf = np.maximum(n, 1).astype(np.float32)
    large = 16 + (np.log(nf / np.float32(16)) / np.float32(math.log(128 / 16)) * np.float32(16)).astype(np.int32)
    large = np.minimum(large, 31)
    return np.where(is_small, n, large)


def build_nc(cfg):
    D, F, NBC, TILE, NSLOT = cfg["D"], cfg["F"], cfg["NBC"], cfg["TILE"], cfg["NSLOT"]
    KC, FC, NH, NKV, QKV = cfg["KC"], cfg["FC"], cfg["NH"], cfg["NKV"], cfg["QKV"]
    GMAX = cfg["GMAX"]
    voff, NV = cfg["voff"], cfg["NV"]
    TT = TILE * 128
    HW = CW - 1
    UCW = HW + TT

    nc = bass.Bass("TRN2", target_bir_lowering=False)
    dt_in = lambda name, shape, dt=F32: nc.dram_tensor(name, shape, dt, kind="ExternalInput").ap()
    x_d = dt_in("x", [NBC * 128, D])
    pw1_d = dt_in("pw1_w", [D, 2 * D])
    pw2_d = dt_in("pw2_w", [D, D])
    wqkv_d = dt_in("wqkv", [D, QKV])
    wo_d = dt_in("wo", [D, D])
    win_d = [dt_in("win%d" % l, [D, 2 * F]) for l in range(2)]
    wout_d = [dt_in("wout%d" % l, [F, D]) for l in range(2)]
    vecs_d = dt_in("vecs", [128, NV])
    pw2b_d = dt_in("pw2b", [1, D])
    sinks_d = dt_in("sinks", [1, NH])
    biasT_d = dt_in("biasT", [128, NKV, 2, 512])
    ident_d = dt_in("ident", [128, 128], BF16)
    y_d = nc.dram_tensor("y", [NBC * 128, D], F32, kind="ExternalOutput").ap()

    with contextlib.ExitStack() as st:
        sb = lambda name, shape, dt: st.enter_context(nc.sbuf_tensor(name, shape, dt))
        UBW = max(KC * UCW, GMAX * TT, NH * TT, TILE * D)
        XR = sb("XR", [128, TILE, D], F32)
        HT = sb("HT", [128, KC, TT], BF16)
        UB = sb("UB", [128, UBW], BF16)
        WS = [sb("ws%d" % i, [128, 4096], BF16) for i in range(NSLOT)]
        VEC = sb("VEC", [128, NV], F32)
        IDT = sb("IDT", [128, 128], BF16)
        ONES = sb("ONES", [128, 128], BF16)
        SINKE = sb("SINKE", [128, NH], F32)
        CCu = sb("CCu", [128, KC, HW], BF16)
        GC = sb("GC", [128, 2 * FC, 2], F32)
        KCr = sb("KCr", [128, NKV, 128], BF16)
        VCr = sb("VCr", [128, NKV * 128], BF16)
        SS = sb("SS", [128, 8], F32)
        F32A = [sb("f32a%d" % i, [128, TT + 2], F32) for i in range(2)]
        F32Bt = sb("f32b", [128, 2, TT], F32)
        F32B = [F32Bt[:, 0, :], F32Bt[:, 1, :]]
        F32C = [sb("f32c%d" % i, [128, TT], F32) for i in range(2)]
        BF2 = [sb("bf2%d" % i, [128, 2, TT], BF16) for i in range(2)]
        QT = sb("QT", [128, 4, TT], BF16)
        NDG = 16
        DG = sb("DG", [128, NDG, 128], BF16)
        dgstate = {"i": 0}
        KTf = sb("KTf", [128, 2 * (128 + TT) // 1], F32)
        VTf = sb("VTf", [128, (TILE + 1) * 256], F32)
        KT = KTf[:].bitcast(BF16).rearrange("p (h t) -> p h t", t=128 + TT)
        VT = VTf[:].bitcast(BF16).rearrange("p (b n) -> p b n", n=512)
        MEAN = VTf[:, 0:TT]
        RSTD = VTf[:, TT:2 * TT]
        BBC = [KTf[:, 0:512], KTf[:, 512:1024]]
        ABI = F32Bt
        PS = [st.enter_context(nc.psum_tensor("ps%d" % i, [128, 512], F32)) for i in range(8)]

        P = Prog()
        B_XR = [Buf("xr%d" % b) for b in range(TILE)]
        B_HT = [Buf("ht%d" % c) for c in range(KC)]
        NUB = (UBW + 511) // 512
        B_UB = [Buf("ub%d" % i) for i in range(NUB)]
        B_WS = [Buf("ws%d" % i) for i in range(NSLOT)]
        B_PS = [Buf("ps%d" % i, excl=True) for i in range(8)]
        B_VEC, B_IDT, B_ONES, B_SINKE = Buf("vec"), Buf("idt"), Buf("ones"), Buf("sinke")
        B_CCu, B_GC, B_KCr, B_VCr = Buf("ccu"), Buf("gc"), Buf("kcr"), Buf("vcr")
        B_SS = Buf("ss")
        B_SSb = [Buf("ss%d" % i) for i in range(TILE)]
        B_A = [Buf("a0"), Buf("a1")]
        B_B = [Buf("b0"), Buf("b1")]
        B_C = [Buf("c0"), Buf("c1")]
        B_BF2 = [Buf("bf20"), Buf("bf21")]
        B_QT, B_KT, B_VT = Buf("qt"), Buf("kt"), Buf("vt")
        B_DG = [Buf("dg%d" % i) for i in range(NDG)]
        B_MEAN = B_RSTD = B_VT
        B_BBC = [B_KT, B_KT]

        def ub_bufs(lo, hi):
            return B_UB[lo // 512:(hi + 511) // 512]

        UBc = UB[:, 0:KC * UCW].rearrange("p (c t) -> p c t", t=UCW)
        UBa = UB[:, 0:GMAX * TT].rearrange("p (c t) -> p c t", t=TT)
        UBo = UB[:, 0:NH * TT].rearrange("p (h t) -> p h t", t=TT)
        UBx = UB[:, 0:TILE * D].rearrange("p (b d) -> p b d", d=D)

        def vcol(name, i):
            o, n = voff[name]
            return VEC[:, o + i:o + i + 1]

        rot = {"a": 0, "b": 0, "c": 0, "bf2": 0, "bbc": 0}

        def nxt(k):
            rot[k] ^= 1
            return rot[k]

        wstate = {"i": 0, "t": 0, "pass": 0}
        NWB = cfg.get("NWB", 3)
        NSCR = cfg.get("NSCR", 720)
        SCRP = 240
        wscr_l = [nc.dram_tensor("wscr%d" % i, [SCRP, 128, 4096], BF16).ap() for i in range((NSCR + SCRP - 1) // SCRP)] if NWB > 0 else []

        def scr_ap(t, n):
            return wscr_l[t // SCRP][t % SCRP, :, 0:n]

        B_SCR = [Buf("scr%d" % i) for i in range(NSCR)] if NWB > 0 else []
        scr_valid = [False] * NSCR

        def wload(src_ap, nk, ncols):
            s = wstate["i"] % NSLOT
            wstate["i"] += 1
            t = wstate["t"]
            wstate["t"] += 1
            n = nk * ncols
            dst = WS[s][:, 0:n].rearrange("p (k n) -> p k n", n=ncols)
            if NWB > 0 and scr_valid[t]:
                P.dma("pool", lambda e: e.dma_start(out=WS[s][:, 0:n], in_=scr_ap(t, n)), reads=[B_SCR[t]], writes=[B_WS[s]])
                return dst, B_WS[s]
            P.dma("pool", lambda e: e.dma_start(out=dst, in_=src_ap), writes=[B_WS[s]])
            if NWB > 0 and wstate["pass"] < NWB and t % NWB == wstate["pass"]:
                assert t < NSCR
                P.dma("sp", lambda e: e.dma_start(out=scr_ap(t, n), in_=WS[s][:, 0:n]), reads=[B_WS[s]], writes=[B_SCR[t]])
                wstate.setdefault("newvalid", []).append(t)
            return dst, B_WS[s]

        def wtile(w_ap, r0, nk, c0, ncols):
            v = w_ap[r0 * 128:(r0 + nk) * 128, c0:c0 + ncols].rearrange("(k p) n -> p k n", p=128)
            return wload(v, nk, ncols)

        P.dma("sp", lambda e: e.dma_start(out=VEC[:], in_=vecs_d), writes=[B_VEC])
        P.dma("sp", lambda e: e.dma_start(out=IDT[:], in_=ident_d), writes=[B_IDT])
        dbg = cfg.get("dbg", 0)
        P.op("dve", lambda e: e.memset(ONES[:], 1.0), writes=[B_ONES])
        if not dbg & 1:
            P.dma("sp", lambda e: e.dma_start(out=SINKE[:], in_=sinks_d.partition_broadcast(128)), writes=[B_SINKE])
            P.op("act", lambda e: e.activation(out=SINKE[:], in_=SINKE[:], func=AF.Exp), reads=[B_SINKE], writes=[B_SINKE])
        if not dbg & 2:
            P.op("dve", lambda e: e.memset(CCu[:], 0.0), writes=[B_CCu])
            P.op("dve", lambda e: e.memset(GC[:], 0.0), writes=[B_GC])
            P.op("dve", lambda e: e.memset(KCr[:], 0.0), writes=[B_KCr])
            P.op("dve", lambda e: e.memset(VCr[:], 0.0), writes=[B_VCr])

        B_ABI = [B_B[0], B_B[1]]

        def ktiles(nchunks, ncols):
            nkmax = max(1, 4096 // ncols)
            out = []
            k = 0
            while k < nchunks:
                n = min(nkmax, nchunks - k)
                out.append((k, n))
                k += n
            return out

        def fm_pieces(w_ap, c0, nch, banks, T):
            pieces = []
            for (k0, nk) in ktiles(KC, nch * 128):
                def piece(k0=k0, nk=nk):
                    wt, wb = wtile(w_ap, k0, nk, c0, nch * 128)
                    for i in range(nch):
                        for k in range(nk):
                            kk = k0 + k
                            P.op("pe", lambda e, o=PS[banks[i]][:, 0:T], l=wt[:, k, i * 128:(i + 1) * 128],
                                 r=HT[:, kk, 0:T], st_=(kk == 0), sp_=(kk == KC - 1): e.matmul(o, l, r, start=st_, stop=sp_),
                                 reads=[wb, B_HT[kk]], writes=[B_PS[banks[i]]])
                pieces.append(piece)
            return pieces

        def fm_matmul(w_ap, c0, nch, banks, T):
            for p in fm_pieces(w_ap, c0, nch, banks, T):
                p()

        def tm_matmul(w_ap, r0, kchunks, lhs_fn, lhs_bufs_fn, nb, bankset, c0, ncols):
            for (k0, nk) in ktiles(kchunks, ncols):
                wt, wb = wtile(w_ap, r0 + k0, nk, c0, ncols)
                for b in range(nb):
                    for k in range(nk):
                        kk = k0 + k
                        P.op("pe", lambda e, o=PS[bankset[b]][:, 0:ncols], l=lhs_fn(kk, b), r=wt[:, k, :],
                             st_=(kk == 0), sp_=(kk == kchunks - 1): e.matmul(o, l, r, start=st_, stop=sp_),
                             reads=[wb] + lhs_bufs_fn(kk), writes=[B_PS[bankset[b]]])

        def rmsnorm_to_HT(nb, gname, layer):
            for b in range(nb):
                junk = UBx[:, b, :]
                jb = ub_bufs(b * D, (b + 1) * D)
                P.op("act", lambda e, o=junk, i=XR[:, b, :], a=SS[:, b:b + 1]: e.activation(
                    out=o, in_=i, func=AF.Square, accum_out=a), reads=[B_XR[b]], writes=jb + [B_SSb[b]])
                P.op("act", lambda e, o=SS[:, 4 + b:5 + b], i=SS[:, b:b + 1]: e.activation(
                    out=o, in_=i, func=AF.Sqrt, scale=1.0 / D, bias=EPS), reads=[B_SSb[b]], writes=[B_SSb[b]])
                P.op("dve", lambda e, o=SS[:, 4 + b:5 + b]: e.reciprocal(o, o), reads=[B_SSb[b]], writes=[B_SSb[b]])
                P.op("dve", lambda e, o=junk, i=XR[:, b, :], r=SS[:, 4 + b:5 + b]: e.tensor_scalar(o, i, r, None, ALU.mult),
                     reads=[B_XR[b], B_SSb[b]], writes=jb)
            for b in range(nb):
                junk = UBx[:, b, :]
                jb = ub_bufs(b * D, (b + 1) * D)
                for c0 in range(0, KC, 8):
                    ncb = min(8, KC - c0)
                    bank = 4 + ((c0 // 8) % 4)
                    pv = PS[bank][:].bitcast(BF16)
                    for c in range(ncb):
                        P.op("pe", lambda e, o=pv[:, c * 128:(c + 1) * 128], i=junk[:, (c0 + c) * 128:(c0 + c + 1) * 128]:
                             e.transpose(o, i, IDT[:]), reads=jb + [B_IDT], writes=[B_PS[bank]])
                    for c in range(ncb):
                        gcol = vcol(gname, layer * KC + c0 + c)
                        o = HT[:, c0 + c, b * 128:(b + 1) * 128]
                        i = pv[:, c * 128:(c + 1) * 128]
                        if ((c0 // 8) + b) % 2 == 0:
                            P.op("act", lambda e, o=o, i=i, g=gcol: e.activation(out=o, in_=i, func=AF.Copy, scale=g),
                                 reads=[B_PS[bank], B_VEC], writes=[B_HT[c0 + c]])
                        else:
                            P.op("dve", lambda e, o=o, i=i, g=gcol: e.tensor_scalar(o, i, g, None, ALU.mult),
                                 reads=[B_PS[bank], B_VEC], writes=[B_HT[c0 + c]])

        def resid_evac(nb, bankset, c0, ncols, bias_ap=None, bias_buf=None):
            for b in range(nb):
                dst = XR[:, b, c0:c0 + ncols]
                P.op("dve", lambda e, d=dst, p=PS[bankset[b]][:, 0:ncols]: e.tensor_tensor(d, p, d, ALU.add),
                     reads=[B_PS[bankset[b]], B_XR[b]], writes=[B_XR[b]])
                if bias_ap is not None:
                    P.op("dve", lambda e, d=dst, bi=bias_ap: e.tensor_tensor(d, d, bi, ALU.add),
                         reads=[B_XR[b], bias_buf], writes=[B_XR[b]])

        def tm_project(w_ap, r0, kchunks, lhs_fn, lhs_bufs_fn, nb, bias=False):
            ncolchunks = D // 512
            for j in range(ncolchunks):
                bankset = [0, 1, 2, 3] if j % 2 == 0 else [4, 5, 6, 7]
                bap = bbuf = None
                if bias:
                    q = nxt("bbc")
                    P.dma("sp", lambda e, o=BBC[q], i=pw2b_d[:, j * 512:(j + 1) * 512].partition_broadcast(128):
                          e.dma_start(out=o, in_=i), writes=[B_BBC[q]])
                    bap, bbuf = BBC[q], B_BBC[q]
                tm_matmul(w_ap, r0, kchunks, lhs_fn, lhs_bufs_fn, nb, bankset, j * 512, 512)
                resid_evac(nb, bankset, j * 512, 512, bap, bbuf)

        def conformer(nb):
            T = nb * 128
            sub = cfg.get("sub", 9)
            if sub >= 1:
                rmsnorm_to_HT(nb, "mixg", 0)
            if sub < 2:
                return
            P.op("act", lambda e: e.activation(out=UBc[:, :, 0:HW], in_=CCu[:, :, :], func=AF.Copy),
                 reads=[B_CCu], writes=ub_bufs(0, KC * UCW))
            ab, gb, cvb = [0, 1], [2, 3], [4, 5]
            groups = [(c0, min(2, KC - c0)) for c0 in range(0, KC, 2)]

            def glu_group(c0, nch):
                fm_matmul(pw1_d, c0 * 128, nch, ab, T)
                fm_matmul(pw1_d, D + c0 * 128, nch, gb, T)
                for i in range(nch):
                    c = c0 + i
                    a = nxt("a")
                    ubb = ub_bufs(c * UCW, (c + 1) * UCW)
                    P.op("act", lambda e, o=F32A[a][:, 0:T], i_=PS[gb[i]][:, 0:T], bi=vcol("pw1b", KC + c): e.activation(
                        out=o, in_=i_, func=AF.Sigmoid, bias=bi), reads=[B_PS[gb[i]], B_VEC], writes=[B_A[a]])
                    P.op("dve", lambda e, o=UBc[:, c, HW:HW + T], p=PS[ab[i]][:, 0:T], bi=vcol("pw1b", c), sg=F32A[a][:, 0:T]:
                         e.scalar_tensor_tensor(o, p, bi, sg, ALU.add, ALU.mult),
                         reads=[B_PS[ab[i]], B_A[a], B_VEC], writes=ubb)

            def conv_group(c0, nch):
                for i in range(nch):
                    c = c0 + i
                    ubb = ub_bufs(c * UCW, (c + 1) * UCW)
                    cb = cvb[i]
                    for k in range(CW):
                        sl = dgstate["i"] % NDG
                        dgstate["i"] += 1
                        P.op("dve", lambda e, o=DG[:, sl, :], w=vcol("dww", c * CW + k): e.tensor_scalar(o, IDT[:], w, None, ALU.mult),
                             reads=[B_IDT, B_VEC], writes=[B_DG[sl]])
                        P.op("pe", lambda e, o=PS[cb][:, 0:T], l=DG[:, sl, :], r=UBc[:, c, k:k + T], st_=(k == 0), sp_=(k == CW - 1):
                             e.matmul(o, l, r, start=st_, stop=sp_), reads=[B_DG[sl]] + ubb, writes=[B_PS[cb]])
                    P.op("act", lambda e, o=CCu[:, c, :], i_=UBc[:, c, T:T + HW]: e.activation(out=o, in_=i_, func=AF.Copy),
                         reads=ubb, writes=[B_CCu])
                    P.op("act", lambda e, o=UBc[:, c, HW:HW + T], i_=PS[cb][:, 0:T], bi=vcol("dwb", c): e.activation(
                        out=o, in_=i_, func=AF.Identity, bias=bi), reads=[B_PS[cb], B_VEC], writes=ubb)
                    q = nxt("bf2")
                    P.op("act", lambda e, o=BF2[q][:, 0, 0:T], i_=PS[cb][:, 0:T], bi=vcol("dwb", c): e.activation(
                        out=o, in_=i_, func=AF.Square, bias=bi), reads=[B_PS[cb], B_VEC], writes=[B_BF2[q]])
                    P.op("pe", lambda e, o=PS[6][:, 0:T], r=UBc[:, c, HW:HW + T], st_=(c == 0), sp_=(c == KC - 1):
                         e.matmul(o, ONES[:], r, start=st_, stop=sp_), reads=ubb + [B_ONES], writes=[B_PS[6]])
                    P.op("pe", lambda e, o=PS[7][:, 0:T], r=BF2[q][:, 0, 0:T], st_=(c == 0), sp_=(c == KC - 1):
                         e.matmul(o, ONES[:], r, start=st_, stop=sp_), reads=[B_BF2[q], B_ONES], writes=[B_PS[7]])

            for j in range(len(groups) + 1):
                if j < len(groups):
                    glu_group(*groups[j])
                if j >= 1:
                    conv_group(*groups[j - 1])
            if sub < 4:
                return
            c_ = nxt("c")
            msq = F32C[c_][:, 0:T]
            mean = MEAN[:, 0:T]
            rstd = RSTD[:, 0:T]
            P.op("dve", lambda e: e.tensor_scalar(mean, PS[6][:, 0:T], 1.0 / D, None, ALU.mult),
                 reads=[B_PS[6]], writes=[B_MEAN])
            P.op("dve", lambda e: e.tensor_tensor(msq, mean, mean, ALU.mult), reads=[B_MEAN], writes=[B_C[c_]])
            P.op("dve", lambda e: e.scalar_tensor_tensor(rstd, PS[7][:, 0:T], 1.0 / D, msq, ALU.mult, ALU.subtract),
                 reads=[B_PS[7], B_C[c_]], writes=[B_RSTD])
            P.op("act", lambda e: e.activation(out=rstd, in_=rstd, func=AF.Sqrt, bias=EPS), reads=[B_RSTD], writes=[B_RSTD])
            P.op("dve", lambda e: e.reciprocal(rstd, rstd), reads=[B_RSTD], writes=[B_RSTD])
            for c in range(KC):
                ubb = ub_bufs(c * UCW, (c + 1) * UCW)
                a = nxt("a")
                t = F32A[a][:, 0:T]
                P.op("dve", lambda e, t=t, v=UBc[:, c, HW:HW + T]: e.tensor_tensor(t, v, mean, ALU.subtract),
                     reads=ubb + [B_MEAN], writes=[B_A[a]])
                P.op("dve", lambda e, t=t: e.tensor_tensor(t, t, rstd, ALU.mult), reads=[B_A[a], B_RSTD], writes=[B_A[a]])
                P.op("act", lambda e, t=t, o=HT[:, c, 0:T], g=vcol("lng", c), bi=vcol("lnb", c): e.activation(
                    out=o, in_=t, func=AF.Silu, scale=g, bias=bi), reads=[B_A[a], B_VEC], writes=[B_HT[c]])
            if sub < 5:
                return
            tm_project(pw2_d, 0, KC, lambda kk, b: HT[:, kk, b * 128:(b + 1) * 128], lambda kk: [B_HT[kk]], nb, bias=True)

        def ffn(nb, l):
            T = nb * 128
            rmsnorm_to_HT(nb, "ffng", l)
            for grp in cfg["ffn_groups"]:
                gch0 = grp[0][0]
                gn = sum(n for _, n in grp)
                for (c0, nch) in grp:
                    fm_matmul(win_d[l], c0 * 128, nch, [0, 1, 2, 3], T)
                    fm_matmul(win_d[l], F + c0 * 128, nch, [4, 5, 6, 7], T)
                    for i in range(nch):
                        c = c0 + i
                        cg = c - gch0
                        a = nxt("a")
                        G = F32A[a]
                        gcr = GC[:, l * FC + c, :]
                        P.op("act", lambda e, o=G[:, 0:2], i_=gcr: e.activation(out=o, in_=i_, func=AF.Copy),
                             reads=[B_GC], writes=[B_A[a]])
                        P.op("act", lambda e, o=G[:, 2:2 + T], i_=PS[i][:, 0:T]: e.activation(out=o, in_=i_, func=AF.Copy),
                             reads=[B_PS[i]], writes=[B_A[a]])
                        P.op("act", lambda e, o=gcr, i_=G[:, T:T + 2]: e.activation(out=o, in_=i_, func=AF.Copy),
                             reads=[B_A[a]], writes=[B_GC])
                        bb = nxt("b")
                        acc = F32B[bb][:, 0:T]
                        wi = (l * FC + c) * FW
                        P.op("dve", lambda e, o=acc, i_=G[:, 2:2 + T], w=vcol("fdw", wi + 2), bi=vcol("fdb", l * FC + c):
                             e.tensor_scalar(o, i_, w, bi, ALU.mult, ALU.add), reads=[B_A[a], B_VEC], writes=[B_B[bb]])
                        P.op("dve", lambda e, o=acc, i_=G[:, 1:1 + T], w=vcol("fdw", wi + 1):
                             e.scalar_tensor_tensor(o, i_, w, o, ALU.mult, ALU.add),
                             reads=[B_A[a], B_VEC, B_B[bb]], writes=[B_B[bb]])
                        P.op("dve", lambda e, o=acc, i_=G[:, 0:T], w=vcol("fdw", wi + 0):
                             e.scalar_tensor_tensor(o, i_, w, o, ALU.mult, ALU.add),
                             reads=[B_A[a], B_VEC, B_B[bb]], writes=[B_B[bb]])
                        cc = nxt("c")
                        S = F32C[cc][:, 0:T]
                        P.op("act", lambda e, o=S, i_=acc: e.activation(out=o, in_=i_, func=AF.Silu),
                             reads=[B_B[bb]], writes=[B_C[cc]])
                        P.op("dve", lambda e, o=UBa[:, cg, 0:T], s_=S, v=PS[4 + i][:, 0:T]: e.tensor_tensor(o, s_, v, ALU.mult),
                             reads=[B_C[cc], B_PS[4 + i]], writes=ub_bufs(cg * TT, (cg + 1) * TT))
                tm_project(wout_d[l], gch0, gn, lambda kk, b: UBa[:, kk, b * 128:(b + 1) * 128],
                           lambda kk: ub_bufs(kk * TT, (kk + 1) * TT), nb)

        def qk_norm(bank, T, gname, dst_ap, dst_bufs, ssbank):
            q = nxt("bf2")
            sq = BF2[q][:, 0, 0:T]
            P.op("act", lambda e: e.activation(out=sq, in_=PS[bank][:, 0:T], func=AF.Square),
                 reads=[B_PS[bank]], writes=[B_BF2[q]])
            P.op("pe", lambda e: e.matmul(PS[ssbank][:, 0:T], ONES[:], sq, start=True, stop=True),
                 reads=[B_BF2[q], B_ONES], writes=[B_PS[ssbank]])
            cc = nxt("c")
            r = F32C[cc][:, 0:T]
            P.op("act", lambda e: e.activation(out=r, in_=PS[ssbank][:, 0:T], func=AF.Ln, scale=1.0 / 128, bias=EPS),
                 reads=[B_PS[ssbank]], writes=[B_C[cc]])
            P.op("act", lambda e: e.activation(out=r, in_=r, func=AF.Exp, scale=-0.5), reads=[B_C[cc]], writes=[B_C[cc]])
            P.op("dve", lambda e: e.scalar_tensor_tensor(dst_ap, PS[bank][:, 0:T], vcol(gname, 0), r, ALU.mult, ALU.mult),
                 reads=[B_PS[bank], B_C[cc], B_VEC], writes=dst_bufs)

        def attention(nb, first_tile):
            T = nb * 128
            scale = 1.0 / math.sqrt(128.0)
            rmsnorm_to_HT(nb, "mixg", 1)
            QTs = [QT, DG[:].rearrange("p a b -> p (a b)").rearrange("p (h t) -> p h t", t=TT)]
            B_QTs = [[B_QT], list(B_DG)]
            rounds = [(r0, min(4, NKV - r0)) for r0 in range(0, NKV, 4)]

            def k_pieces(r0, nkv):
                return fm_pieces(wqkv_d, NH * 128 + r0 * 128, nkv, [0, 1, 2, 3], T)

            def k_finish(r0, nkv):
                P.op("act", lambda e, o=KT[:, 0:nkv, 0:128], i_=KCr[:, r0:r0 + nkv, :]: e.activation(out=o, in_=i_, func=AF.Copy),
                     reads=[B_KCr], writes=[B_KT])
                for i in range(nkv):
                    qk_norm(i, T, "kg", KT[:, i, 128:128 + T], [B_KT], 7)
                P.op("act", lambda e, o=KCr[:, r0:r0 + nkv, :], i_=KT[:, 0:nkv, T:T + 128]: e.activation(out=o, in_=i_, func=AF.Copy),
                     reads=[B_KT], writes=[B_KCr])

            def v_all(r0, nkv):
                P.op("act", lambda e, o=VT[:, 0, 0:nkv * 128], i_=VCr[:, r0 * 128:(r0 + nkv) * 128]: e.activation(out=o, in_=i_, func=AF.Copy),
                     reads=[B_VCr], writes=[B_VT])
                tm_matmul(wqkv_d, 0, KC, lambda kk, b: HT[:, kk, b * 128:(b + 1) * 128], lambda kk: [B_HT[kk]],
                          nb, [0, 1, 2, 3], (NH + NKV) * 128 + r0 * 128, nkv * 128)
                for b in range(nb):
                    P.op("act", lambda e, o=VT[:, 1 + b, 0:nkv * 128], i_=PS[b][:, 0:nkv * 128]: e.activation(out=o, in_=i_, func=AF.Copy),
                         reads=[B_PS[b]], writes=[B_VT])
                P.op("act", lambda e, o=VCr[:, r0 * 128:(r0 + nkv) * 128], i_=VT[:, nb, 0:nkv * 128]: e.activation(out=o, in_=i_, func=AF.Copy),
                     reads=[B_VT], writes=[B_VCr])

            def q_pieces(kv):
                return fm_pieces(wqkv_d, kv * 512, 4, [0, 1, 2, 3], T)

            def q_finish(kv):
                for i in range(4):
                    qk_norm(i, T, "qg", QTs[kv % 2][:, i, 0:T], B_QTs[kv % 2], 7)

            def attn_blocks(kv, jj, fillers):
                QTc, B_QTc = QTs[kv % 2], B_QTs[kv % 2]
                nf = len(fillers)
                per = [(nf + nb - 1 - b) // nb for b in range(nb)]
                for b in range(nb):
                    has_prev = not (first_tile and b == 0)
                    rhs = QTc[:, :, b * 128:(b + 1) * 128]
                    q = nxt("bf2")
                    parts = ([0] if has_prev else []) + [1]
                    for w in parts:
                        ko = b * 128 if w == 0 else 128 + b * 128
                        sbank = 4 + w
                        P.op("pe", lambda e, o=PS[sbank][:, :], l=KT[:, jj, ko:ko + 128], r=rhs: e.matmul(o, l, r, start=True, stop=True),
                             reads=[B_KT] + B_QTc, writes=[B_PS[sbank]])
                        a = nxt("a")
                        t = F32A[a][:, 0:512]
                        P.op("dve", lambda e, t=t, p=PS[sbank][:, :], bi=ABI[:, w, :]: e.scalar_tensor_tensor(
                            t, p, scale, bi, ALU.mult, ALU.add), reads=[B_PS[sbank]] + B_ABI, writes=[B_A[a]])
                        P.op("act", lambda e, t=t, o=BF2[q][:, w, :]: e.activation(out=o, in_=t, func=AF.Exp),
                             reads=[B_A[a]], writes=[B_BF2[q]])
                    for _ in range(per[b]):
                        fillers.pop(0)()
                    for wi_, w in enumerate(parts):
                        P.op("pe", lambda e, r=BF2[q][:, w, :], st_=(wi_ == 0), sp_=(wi_ == len(parts) - 1):
                             e.matmul(PS[6][:, :], ONES[:], r, start=st_, stop=sp_),
                             reads=[B_BF2[q], B_ONES], writes=[B_PS[6]])
                    for wi_, w in enumerate(parts):
                        vb = b if w == 0 else b + 1
                        P.op("pe", lambda e, l=VT[:, vb, jj * 128:(jj + 1) * 128], r=BF2[q][:, w, :],
                             st_=(wi_ == 0), sp_=(wi_ == len(parts) - 1): e.matmul(PS[7][:, :], l, r, start=st_, stop=sp_),
                             reads=[B_BF2[q], B_VT], writes=[B_PS[7]])
                    cc = nxt("c")
                    rc = F32C[cc][:, 0:512]
                    for i in range(4):
                        P.op("act", lambda e, o=rc[:, i * 128:(i + 1) * 128], p=PS[6][:, i * 128:(i + 1) * 128],
                             sk=SINKE[:, kv * 4 + i:kv * 4 + i + 1]: e.activation(out=o, in_=p, func=AF.Ln, bias=sk),
                             reads=[B_PS[6], B_SINKE], writes=[B_C[cc]])
                    P.op("act", lambda e, rc=rc: e.activation(out=rc, in_=rc, func=AF.Exp, scale=-1.0), reads=[B_C[cc]], writes=[B_C[cc]])
                    dst = UBo[:, kv * 4:kv * 4 + 4, b * 128:(b + 1) * 128]
                    P.op("dve", lambda e, rc=rc, dst=dst: e.tensor_tensor(
                        dst, PS[7][:, :].rearrange("p (h t) -> p h t", t=128),
                        rc.rearrange("p (h t) -> p h t", t=128), ALU.mult),
                        reads=[B_PS[7], B_C[cc]], writes=ub_bufs(kv * 4 * TT, (kv * 4 + 4) * TT))
                while fillers:
                    fillers.pop(0)()

            for p in k_pieces(*rounds[0]):
                p()
            k_finish(*rounds[0])
            v_all(*rounds[0])
            for p in q_pieces(0):
                p()
            q_finish(0)
            for kv in range(NKV):
                ri, jj = divmod(kv, 4)
                P.dma("sp", lambda e, i_=biasT_d[:, kv, :, :]: e.dma_start(out=ABI[:], in_=i_), writes=B_ABI)
                fillers = []
                nxt_round = (kv + 1 < NKV) and ((kv + 1) % 4 == 0)
                if kv + 1 < NKV:
                    fillers = k_pieces(*rounds[ri + 1]) if nxt_round else q_pieces(kv + 1)
                attn_blocks(kv, jj, fillers)
                if kv + 1 < NKV:
                    if nxt_round:
                        k_finish(*rounds[ri + 1])
                        v_all(*rounds[ri + 1])
                        for p in q_pieces(kv + 1):
                            p()
                    q_finish(kv + 1)
            tm_project(wo_d, 0, NH, lambda kk, b: UBo[:, kk, b * 128:(b + 1) * 128],
                       lambda kk: ub_bufs(kk * TT, (kk + 1) * TT), nb)

        stores = []
        blk = 0
        first = True
        stages = cfg.get("stages", 4)
        ntiles = (NBC + TILE - 1) // TILE
        sched = [NBC // ntiles + (1 if i < NBC % ntiles else 0) for i in range(ntiles)]
        for nb in sched:
            for b in range(nb):
                P.dma("sp", lambda e, o=XR[:, b, :], i_=x_d[(blk + b) * 128:(blk + b + 1) * 128, :]: e.dma_start(out=o, in_=i_),
                      writes=[B_XR[b]])
            if stages >= 1:
                conformer(nb)
            if stages >= 2:
                ffn(nb, 0)
            if stages >= 3:
                attention(nb, first)
            if stages >= 4:
                ffn(nb, 1)
            for b in range(nb):
                stores.append(P.dma("sp", lambda e, o=y_d[(blk + b) * 128:(blk + b + 1) * 128, :], i_=XR[:, b, :]:
                                    e.dma_start(out=o, in_=i_), reads=[B_XR[b]]))
            blk += nb
            first = False
            for t_ in wstate.pop("newvalid", []):
                scr_valid[t_] = True
            wstate["pass"] += 1
            wstate["t"] = 0
        P.barrier_wait("sp", stores)
        P.emit(nc)
    return nc


def fm(v):
    v = np.asarray(v, dtype=np.float32)
    lead = v.shape[:-1]
    C = v.shape[-1] // 128
    a = v.reshape(lead + (C, 128))
    a = np.moveaxis(a, -1, 0)
    return a


def prep_shared(cfg, inp):
    D, F, KC, FC, NH, NKV = cfg["D"], cfg["F"], cfg["KC"], cfg["FC"], cfg["NH"], cfg["NKV"]
    NV, voff = cfg["NV"], cfg["voff"]
    vecs = np.zeros((128, NV), np.float32)

    def put(name, arr):
        o, n = voff[name]
        vecs[:, o:o + n] = arr.reshape(128, n)
    put("mixg", fm(inp["mix_norm_g"]))
    put("ffng", fm(inp["ffn_norm_g"]))
    put("pw1b", fm(inp["conv_pw1_b"][0]))
    put("dww", np.transpose(fm(inp["conv_dw_w"][0]), (0, 2, 1)))
    put("dwb", fm(inp["conv_dw_b"][0]))
    put("lng", fm(inp["conv_ln_g"][0]))
    put("lnb", fm(inp["conv_ln_b"][0]))
    put("fdw", np.transpose(fm(inp["ffn_dw_w"]), (0, 1, 3, 2)))
    put("fdb", fm(inp["ffn_dw_b"]))
    put("qg", np.asarray(inp["attn_q_norm_g"][0], np.float32).reshape(128, 1))
    put("kg", np.asarray(inp["attn_k_norm_g"][0], np.float32).reshape(128, 1))
    rb = np.asarray(inp["rel_bias"], np.float32)
    tbl = np.concatenate([rb, np.full((1, NH), NEG, np.float32)], axis=0)
    qi = np.arange(128)[None, :]
    kj = np.arange(128)[:, None]
    idx = np.zeros((2, 128, 128), np.int64)
    d_prev = qi - kj + 128
    d_cur = qi - kj
    idx[0] = np.where((d_prev >= 0) & (d_prev < 128), t5_bucket(d_prev), 32)
    idx[1] = np.where((d_cur >= 0) & (d_cur < 128), t5_bucket(d_cur), 32)
    g = tbl[idx]
    g = g.reshape(2, 128, 128, NKV, 4)
    biasT = np.ascontiguousarray(np.transpose(g, (1, 3, 0, 4, 2))).reshape(128, NKV, 2, 512)
    shared = {
        "pw1_w": np.ascontiguousarray(inp["conv_pw1_w"][0], dtype=np.float32),
        "pw2_w": np.ascontiguousarray(inp["conv_pw2_w"][0], dtype=np.float32),
        "wqkv": np.ascontiguousarray(inp["attn_w_qkv"][0], dtype=np.float32),
        "wo": np.ascontiguousarray(inp["attn_w_o"][0], dtype=np.float32),
        "win0": np.ascontiguousarray(inp["ffn_w_in"][0], dtype=np.float32),
        "win1": np.ascontiguousarray(inp["ffn_w_in"][1], dtype=np.float32),
        "wout0": np.ascontiguousarray(inp["ffn_w_out"][0], dtype=np.float32),
        "wout1": np.ascontiguousarray(inp["ffn_w_out"][1], dtype=np.float32),
        "vecs": vecs,
        "pw2b": np.ascontiguousarray(inp["conv_pw2_b"][0], dtype=np.float32).reshape(1, D),
        "sinks": np.ascontiguousarray(inp["attn_sinks"][0], dtype=np.float32).reshape(1, NH),
        "biasT": biasT.astype(np.float32),
        "ident": np.eye(128, dtype=np.float32).astype(ml_dtypes.bfloat16),
    }
    return shared


def core_starts(nblk_total, nbc, ncores):
    step = (nblk_total) // ncores
    return [min(step * c, nblk_total - nbc) for c in range(ncores)]


_NC_CACHE = {}


def run_cfg(cfg, inp, ncores):
    D, NBC = cfg["D"], cfg["NBC"]
    x = np.asarray(inp["x"], dtype=np.float32)
    S = x.shape[1]
    nblk = S // 128
    starts = core_starts(nblk, NBC, ncores)
    shared = prep_shared(cfg, inp)
    in_maps = []
    for c in range(ncores):
        m = dict(shared)
        m["x"] = np.ascontiguousarray(x[0, starts[c] * 128:(starts[c] + NBC) * 128, :])
        in_maps.append(m)
    key = (D, cfg["F"], NBC, cfg["TILE"], cfg["NSLOT"])
    if key not in _NC_CACHE:
        _NC_CACHE[key] = build_nc(cfg)
    nc = _NC_CACHE[key]
    res = run_bass_kernel_spmd(nc, in_maps, core_ids=list(range(ncores)))
    out = np.empty((1, S, D), np.float32)
    for c in range(ncores):
        y = res.results[c]["y"]
        lo = 0 if starts[c] == 0 else 2
        out[0, (starts[c] + lo) * 128:(starts[c] + NBC) * 128, :] = y[lo * 128:, :]
    return out


def kernel(**inputs):
    return run_cfg(FULL, inputs, 8)
```
